# Optimizing a Trainium2 kernel written in Bass

```python
import math, functools
import jax, jax.numpy as jnp
from jax import lax
import numpy as np

D_MODEL = 1024
BATCH = 4
SEQ = 4096
DEPTH = 2

GRID_W = 64
CTX_LEN = 256
N_MOD = 6
NORM_EPS = 1e-6
D_FF = -(-8 * D_MODEL // (3 * 256)) * 256

D_MIX = D_MODEL
GLA_HEADS = 4
GLA_DV = D_MIX // 2 // GLA_HEADS
GLA_DK = GLA_DV // 2
GLA_LR = 16
GLA_TAU = 16.0
GLA_CHUNK = 64
GLA_HK = GLA_HEADS * GLA_DK
GLA_HV = GLA_HEADS * GLA_DV
GMLP_GROUPS = 4
GMLP_WIDTH = D_MIX // 2
GMLP_GC = GMLP_WIDTH // GMLP_GROUPS
GMLP_CHUNK = 128
AB_STATE_SPLITS = (GLA_HK, GLA_HV, GLA_LR, GLA_LR)
AB_OUT_SPLITS = (GLA_HK, GLA_HV, GMLP_WIDTH, GMLP_WIDTH)
AB_IN = sum(AB_STATE_SPLITS + AB_OUT_SPLITS)

SSD_INNER = 2 * D_MODEL
SSD_HEADDIM = 64
SSD_HEADS = SSD_INNER // SSD_HEADDIM
SSD_GROUPS = 4
SSD_HPG = SSD_HEADS // SSD_GROUPS
SSD_STATE = 128
SSD_CHUNK = 128
SSD_CONV = 5
SSD_GS = SSD_GROUPS * SSD_STATE
SSD_CONV_DIM = SSD_INNER + 2 * SSD_GS
SSD_IN = SSD_CONV_DIM + 2 * SSD_HEADS + SSD_INNER

kernel_name = 'hybrid_gla_gmlp_ssd_prefix_dit'


def rms_norm(x, g):
    xf = x.astype(jnp.float32)
    y = xf * lax.rsqrt(jnp.mean(xf * xf, axis=-1, keepdims=True) + NORM_EPS)
    return y.astype(x.dtype) * g


def group_rms_norm(x, g, groups):
    shp = x.shape
    xf = x.astype(jnp.float32).reshape(shp[:-1] + (groups, shp[-1] // groups))
    y = xf * lax.rsqrt(jnp.mean(xf * xf, axis=-1, keepdims=True) + NORM_EPS)
    return y.reshape(shp).astype(x.dtype) * g


def layer_norm(x, g):
    xf = x.astype(jnp.float32)
    mu = jnp.mean(xf, axis=-1, keepdims=True)
    var = jnp.mean(jnp.square(xf - mu), axis=-1, keepdims=True)
    return ((xf - mu) * lax.rsqrt(var + NORM_EPS)).astype(x.dtype) * g


def modulate(h, shift, scale):
    return h * (1 + scale) + shift


def split_cols(a, sizes):
    idx = [int(i) for i in np.cumsum(sizes)[:-1]]
    return jnp.split(a, idx, axis=-1)


def swiglu(h, w_in, w_out):
    g, u = jnp.split(h @ w_in, 2, axis=-1)
    return (jax.nn.silu(g) * u) @ w_out


def to_col_major(x):
    bsz, t, d = x.shape
    rows = t // GRID_W
    return x.reshape(bsz, rows, GRID_W, d).transpose(0, 2, 1, 3).reshape(bsz, t, d)


def to_row_major(x):
    bsz, t, d = x.shape
    rows = t // GRID_W
    return x.reshape(bsz, GRID_W, rows, d).transpose(0, 2, 1, 3).reshape(bsz, t, d)


def dwconv_centred(x, w, b):
    pad = (w.shape[0] - 1) // 2
    y = lax.conv_general_dilated(x, w.astype(x.dtype)[:, None, :], window_strides=(1,),
                                 padding=[(pad, pad)], dimension_numbers=('NWC', 'WIO', 'NWC'),
                                 feature_group_count=x.shape[-1])
    return y + b


def bidir_prefix_scan(scan_f, scan_b, ctx_f, ctx_b, lat_f, lat_b, s0, ctx_out):
    flip = lambda a: None if a is None else jnp.flip(a, axis=1)
    y_cf, s_cf = scan_f(*ctx_f, s0, ctx_out)
    y_xf, _ = scan_f(*lat_f, s_cf, True)
    y_cb, s_cb = scan_b(*[flip(a) for a in ctx_b], s0, ctx_out)
    y_xb, _ = scan_b(*[flip(a) for a in lat_b], s_cb, True)
    y_x = y_xf + flip(y_xb)
    y_c = y_cf + flip(y_cb) if ctx_out else None
    return y_x, y_c


def gla_chunked(k, v, log_a, q, s0, with_output):
    bsz, t, h, dk = k.shape
    dv = v.shape[-1]
    n = t // GLA_CHUNK
    chunks = lambda a: a.astype(jnp.float32).reshape(bsz, n, GLA_CHUNK, h, a.shape[-1])
    k, v, log_a = chunks(k), chunks(v), chunks(log_a)
    b = jnp.cumsum(log_a, axis=2)
    b_last = b[:, :, -1]
    d_state = jnp.einsum('bnlhk,bnlhv->bnhkv', k * jnp.exp(b_last[:, :, None] - b), v)

    def step(s, inp):
        decay, ds = inp
        return decay[..., None] * s + ds, s

    s_final, s_prev = lax.scan(step, s0, (jnp.moveaxis(jnp.exp(b_last), 1, 0), jnp.moveaxis(d_state, 1, 0)))
    if not with_output:
        return None, s_final
    q = chunks(q)
    s_prev = jnp.moveaxis(s_prev, 0, 1)
    q_dec = q * jnp.exp(b)
    o_inter = jnp.einsum('bnlhk,bnhkv->bnlhv', q_dec, s_prev)
    scores = jnp.einsum('bnlhk,bnshk->bnhls', q_dec, k * jnp.exp(-b))
    tri = jnp.tril(jnp.ones((GLA_CHUNK, GLA_CHUNK), dtype=bool))
    scores = jnp.where(tri, scores, 0.0)
    o_intra = jnp.einsum('bnhls,bnshv->bnlhv', scores, v)
    return (o_inter + o_intra).reshape(bsz, t, h, dv), s_final


def ssd_chunked(x, bm, dt, cm, s0, with_output, a_coef):
    bsz, t = x.shape[:2]
    n = t // SSD_CHUNK
    f = lambda z: z.astype(jnp.float32).reshape((bsz, n, SSD_CHUNK) + z.shape[2:])
    x, bm, dt = f(x), f(bm), f(dt)
    acum = jnp.cumsum(dt * a_coef, axis=2)
    a_last = acum[:, :, -1]
    xdt = x * dt[..., None]
    states = jnp.einsum('bclgn,bclgh,bclghp->bcghpn', bm, jnp.exp(a_last[:, :, None] - acum), xdt)

    def step(s, inp):
        decay, st = inp
        return decay[..., None, None] * s + st, s

    s_final, s_prev = lax.scan(step, s0, (jnp.moveaxis(jnp.exp(a_last), 1, 0), jnp.moveaxis(states, 1, 0)))
    if not with_output:
        return None, s_final
    cm = f(cm)
    s_prev = jnp.moveaxis(s_prev, 0, 1)
    y_off = jnp.einsum('bclgn,bcghpn->bclghp', cm, s_prev) * jnp.exp(acum)[..., None]
    at = jnp.moveaxis(acum, 2, -1)
    seg = at[..., :, None] - at[..., None, :]
    tri = jnp.tril(jnp.ones((SSD_CHUNK, SSD_CHUNK), dtype=bool))
    decay = jnp.exp(jnp.where(tri, seg, -jnp.inf))
    cb = jnp.einsum('bclgn,bcsgn->bcgls', cm, bm)
    y_diag = jnp.einsum('bcgls,bcghls,bcsghp->bclghp', cb, decay, xdt)
    return (y_diag + y_off).reshape((bsz, t) + x.shape[3:]), s_final


def ab_project(h, w_in, gate_w, gate_b, full):
    bsz, t, _ = h.shape
    sizes = AB_STATE_SPLITS + AB_OUT_SPLITS if full else AB_STATE_SPLITS
    parts = split_cols(h @ w_in[:, :sum(sizes)], sizes)
    heads = lambda a: a.reshape(bsz, t, GLA_HEADS, -1)
    k, v, lr_f, lr_b = parts[:4]
    la_f = jax.nn.log_sigmoid((lr_f @ gate_w[0] + gate_b[0]).astype(jnp.float32)) / GLA_TAU
    la_b = jax.nn.log_sigmoid((lr_b @ gate_w[1] + gate_b[1]).astype(jnp.float32)) / GLA_TAU
    state = (heads(k), heads(v), heads(la_f), heads(la_b))
    if not full:
        return state, None
    q, r, u, g = parts[4:]
    return state, (heads(q) * GLA_DK ** -0.5, r, u, g)


def gmlp_chunk_mix(u, v, vnorm_g, spatial_w, spatial_b):
    bsz, t, _ = u.shape
    n = t // GMLP_CHUNK
    u = jax.nn.gelu(u)
    v = layer_norm(jax.nn.gelu(v), vnorm_g).reshape(bsz, n, GMLP_CHUNK, GMLP_GROUPS, GMLP_GC)
    s = jnp.einsum('gts,bnsgc->bntgc', spatial_w, v) + spatial_b.T[:, :, None]
    return u * s.reshape(bsz, t, GMLP_WIDTH)


def mixer_gla_gmlp(hx, hc, w_in, gate_w, gate_b, gla_norm_g, vnorm_g, spatial_w, spatial_b, w_out, ctx_out):
    (kx, vx, lfx, lbx), (qx, rx, ux, gx) = ab_project(hx, w_in, gate_w, gate_b, True)
    (kc, vc, lfc, lbc), rest_c = ab_project(hc, w_in, gate_w, gate_b, ctx_out)
    qc = rest_c[0] if ctx_out else None
    s0 = jnp.zeros((hx.shape[0], GLA_HEADS, GLA_DK, GLA_DV), jnp.float32)
    ox, oc = bidir_prefix_scan(gla_chunked, gla_chunked,
                               [kc, vc, lfc, qc], [kc, vc, lbc, qc],
                               [kx, vx, lfx, qx], [kx, vx, lbx, qx], s0, ctx_out)

    def merge(o, r, u, g):
        bsz, t = o.shape[:2]
        a = group_rms_norm(o.reshape(bsz, t, GLA_HV), gla_norm_g, GLA_HEADS) * jax.nn.silu(r)
        b = gmlp_chunk_mix(u, g, vnorm_g, spatial_w, spatial_b)
        return jnp.concatenate([a.astype(b.dtype), b], axis=-1) @ w_out

    yx = merge(ox, rx, ux, gx)
    yc = merge(oc, rest_c[1], rest_c[2], rest_c[3]) if ctx_out else None
    return yx, yc


def ssd_project(h, w_in, conv_w, conv_b, dt_bias, full):
    bsz, t, _ = h.shape
    if full:
        p = h @ w_in
        xbc = jax.nn.silu(dwconv_centred(p[..., :SSD_CONV_DIM], conv_w, conv_b))
        xs, bm, cm = split_cols(xbc, (SSD_INNER, SSD_GS, SSD_GS))
        dt_raw = p[..., SSD_CONV_DIM:SSD_CONV_DIM + 2 * SSD_HEADS]
        z = p[..., SSD_CONV_DIM + 2 * SSD_HEADS:]
        cm = cm.reshape(bsz, t, SSD_GROUPS, SSD_STATE)
    else:
        nxb = SSD_INNER + SSD_GS
        xb = jax.nn.silu(dwconv_centred(h @ w_in[:, :nxb], conv_w[:, :nxb], conv_b[:nxb]))
        xs, bm = split_cols(xb, (SSD_INNER, SSD_GS))
        dt_raw = h @ w_in[:, SSD_CONV_DIM:SSD_CONV_DIM + 2 * SSD_HEADS]
        cm = None
        z = None
    dt = jax.nn.softplus(dt_raw.astype(jnp.float32).reshape(bsz, t, 2, SSD_HEADS) + dt_bias)
    dt = dt.reshape(bsz, t, 2, SSD_GROUPS, SSD_HPG)
    xs = xs.reshape(bsz, t, SSD_GROUPS, SSD_HPG, SSD_HEADDIM)
    bm = bm.reshape(bsz, t, SSD_GROUPS, SSD_STATE)
    return xs, bm, cm, dt[:, :, 0], dt[:, :, 1], z


def mixer_ssd(hx, hc, w_in, conv_w, conv_b, dt_bias, a_log, d_skip, norm_g, w_out, ctx_out):
    xx, bx, cx, dfx, dbx, zx = ssd_project(hx, w_in, conv_w, conv_b, dt_bias, True)
    xc, bc, cc, dfc, dbc, zc = ssd_project(hc, w_in, conv_w, conv_b, dt_bias, ctx_out)
    a = -jnp.exp(a_log.astype(jnp.float32)).reshape(2, SSD_GROUPS, SSD_HPG)
    s0 = jnp.zeros((hx.shape[0], SSD_GROUPS, SSD_HPG, SSD_HEADDIM, SSD_STATE), jnp.float32)
    yx, yc = bidir_prefix_scan(functools.partial(ssd_chunked, a_coef=a[0]),
                               functools.partial(ssd_chunked, a_coef=a[1]),
                               [xc, bc, dfc, cc], [xc, bc, dbc, cc],
                               [xx, bx, dfx, cx], [xx, bx, dbx, cx], s0, ctx_out)
    d_h = d_skip.reshape(SSD_GROUPS, SSD_HPG)[..., None]

    def finish(y, xs, z):
        bsz, t = y.shape[:2]
        y = (y + d_h * xs).reshape(bsz, t, SSD_INNER)
        y = group_rms_norm(y * jax.nn.silu(z), norm_g, SSD_GROUPS)
        return y.astype(z.dtype) @ w_out

    out_x = finish(yx, xx, zx)
    out_c = finish(yc, xc, zc) if ctx_out else None
    return out_x, out_c


def setup_inputs(seed: int = 0) -> dict:
    key = jax.random.key(seed)
    ks = iter(jax.random.split(key, 40))

    def nrm(shape, scale):
        return jax.random.normal(next(ks), shape, jnp.float32) * scale

    ne, no = (DEPTH + 1) // 2, DEPTH // 2
    dt0 = jnp.exp(jax.random.uniform(next(ks), (no, 2, SSD_HEADS), jnp.float32, math.log(1e-3), math.log(1e-1)))
    return {
        'x': nrm((BATCH, SEQ, D_MODEL), 1.0),
        'c': nrm((BATCH, D_MODEL), 1.0),
        'ctx': nrm((BATCH, CTX_LEN, D_MODEL), 1.0),
        'c_ctx': nrm((D_MODEL,), 1.0),
        'mod_w': nrm((DEPTH, D_MODEL, N_MOD * D_MODEL), D_MODEL ** -0.5),
        'mod_b': nrm((DEPTH, N_MOD * D_MODEL), 0.02),
        'norm_g': 1.0 + nrm((DEPTH, 2, D_MODEL), 0.02),
        'ffn_w_in': nrm((DEPTH, D_MODEL, 2 * D_FF), D_MODEL ** -0.5),
        'ffn_w_out': nrm((DEPTH, D_FF, D_MODEL), D_FF ** -0.5),
        'ab_w_in': nrm((ne, D_MODEL, AB_IN), D_MODEL ** -0.5),
        'ab_gate_w': nrm((ne, 2, GLA_LR, GLA_HK), GLA_LR ** -0.5),
        'ab_gate_b': nrm((ne, 2, GLA_HK), 0.1),
        'ab_gla_norm_g': 1.0 + nrm((ne, GLA_HV), 0.02),
        'ab_vnorm_g': 1.0 + nrm((ne, GMLP_WIDTH), 0.02),
        'ab_spatial_w': nrm((ne, GMLP_GROUPS, GMLP_CHUNK, GMLP_CHUNK), GMLP_CHUNK ** -0.5),
        'ab_spatial_b': 1.0 + nrm((ne, GMLP_GROUPS, GMLP_CHUNK), 0.02),
        'ab_w_out': nrm((ne, D_MIX, D_MODEL), D_MIX ** -0.5),
        'ssd_w_in': nrm((no, D_MODEL, SSD_IN), D_MODEL ** -0.5),
        'ssd_conv_w': nrm((no, SSD_CONV, SSD_CONV_DIM), SSD_CONV ** -0.5),
        'ssd_conv_b': nrm((no, SSD_CONV_DIM), 0.02),
        'ssd_dt_bias': dt0 + jnp.log(-jnp.expm1(-dt0)),
        'ssd_a_log': jnp.log(jax.random.uniform(next(ks), (no, 2, SSD_HEADS), jnp.float32, 1.0, 16.0)),
        'ssd_d': 1.0 + nrm((no, SSD_HEADS), 0.02),
        'ssd_norm_g': 1.0 + nrm((no, SSD_INNER), 0.02),
        'ssd_w_out': nrm((no, SSD_INNER, D_MODEL), SSD_INNER ** -0.5),
        'final_norm_g': 1.0 + nrm((D_MODEL,), 0.02),
    }


def reference(x, c, ctx, c_ctx, mod_w, mod_b, norm_g, ffn_w_in, ffn_w_out,
              ab_w_in, ab_gate_w, ab_gate_b, ab_gla_norm_g, ab_vnorm_g, ab_spatial_w, ab_spatial_b, ab_w_out,
              ssd_w_in, ssd_conv_w, ssd_conv_b, ssd_dt_bias, ssd_a_log, ssd_d, ssd_norm_g, ssd_w_out,
              final_norm_g):
    sc = jax.nn.silu(c)
    sc_ctx = jax.nn.silu(c_ctx)
    for i in range(DEPTH):
        ctx_out = i < DEPTH - 1
        j = i // 2
        mx = jnp.split((sc @ mod_w[i] + mod_b[i])[:, None, :], N_MOD, axis=-1)
        mc = jnp.split(sc_ctx @ mod_w[i] + mod_b[i], N_MOD, axis=-1)
        hx = modulate(rms_norm(x, norm_g[i, 0]), mx[0], mx[1])
        hc = modulate(rms_norm(ctx, norm_g[i, 0]), mc[0], mc[1])
        if i % 2 == 0:
            yx, yc = mixer_gla_gmlp(hx, hc, ab_w_in[j], ab_gate_w[j], ab_gate_b[j], ab_gla_norm_g[j],
                                    ab_vnorm_g[j], ab_spatial_w[j], ab_spatial_b[j], ab_w_out[j], ctx_out)
        else:
            yx_cm, yc = mixer_ssd(to_col_major(hx), hc, ssd_w_in[j], ssd_conv_w[j], ssd_conv_b[j],
                                  ssd_dt_bias[j], ssd_a_log[j], ssd_d[j], ssd_norm_g[j], ssd_w_out[j], ctx_out)
            yx = to_row_major(yx_cm)
        x = x + mx[2] * yx
        x = x + mx[5] * swiglu(modulate(rms_norm(x, norm_g[i, 1]), mx[3], mx[4]), ffn_w_in[i], ffn_w_out[i])
        if ctx_out:
            ctx = ctx + mc[2] * yc
            ctx = ctx + mc[5] * swiglu(modulate(rms_norm(ctx, norm_g[i, 1]), mc[3], mc[4]), ffn_w_in[i], ffn_w_out[i])
    return rms_norm(x, final_norm_g)
```

```python
import contextlib
import numpy as np
import concourse.bass as bass
import concourse.mybir as mybir
from concourse.bass_utils import run_bass_kernel_spmd

F32 = mybir.dt.float32
BF16 = mybir.dt.bfloat16
AF = mybir.ActivationFunctionType
ALU = mybir.AluOpType
AX = mybir.AxisListType

ENGS = ("pe", "act", "dve", "pool", "sp")
SAME_ENGINE_SYNC = True
N_DMA_SEMS = 12


class U:
    __slots__ = ("name", "w", "rs", "excl", "ser")

    def __init__(self, name, excl=False):
        self.name = name
        self.w = None
        self.rs = []
        self.excl = excl
        self.ser = 0


class T:
    __slots__ = ("ap", "us", "ser")

    def __init__(self, ap, us, ser=None):
        self.ap = ap
        self.us = us if isinstance(us, (list, tuple)) else [us]
        self.ser = ser

    def __getitem__(self, key):
        return T(self.ap[key], self.us, self.ser)

    def v(self, ap):
        return T(ap, self.us, self.ser)


class Op:
    __slots__ = ("eng", "idx", "fn", "deps", "dma", "sig", "waits", "dsem", "dval", "blk", "inc")

    def __init__(self, eng, idx, fn, deps, dma):
        self.eng = eng
        self.idx = idx
        self.fn = fn
        self.deps = deps
        self.dma = dma
        self.sig = None
        self.waits = []
        self.dsem = None
        self.dval = None


class Prog:
    def __init__(self, nc, stack):
        self.nc = nc
        self.stack = stack
        self.esem = {e: stack.enter_context(nc.semaphore("es_" + e)) for e in ENGS if e != "sp"}
        self.ecount = {e: 0 for e in ENGS}
        self.dsems = [stack.enter_context(nc.semaphore("ds%d" % i)) for i in range(N_DMA_SEMS)]
        self.dcount = [0] * N_DMA_SEMS
        self.dlast = [None] * N_DMA_SEMS
        self.dnext = 0
        self.pending = {e: [] for e in ENGS}
        self.all_ops = []
        self.waited = {e: {} for e in ENGS}
        self.uid = 0
        self.nblocks = 0

    def sb(self, stack, name, shape, dtype, nunits=1):
        self.uid += 1
        name = "%s_%d" % (name, self.uid)
        t = stack.enter_context(self.nc.sbuf_tensor(name, list(shape), dtype))
        return T(t[:] if hasattr(t, "__getitem__") else t.ap(), U(name))

    def ps(self, stack, name, shape, dtype):
        t = stack.enter_context(self.nc.psum_tensor(name, list(shape), dtype))
        return T(t[:], U(name, excl=True))

    def add(self, eng, fn, reads=(), writes=(), dma=False):
        for t in list(reads) + list(writes):
            if t.ser is not None and t.us[0].ser != t.ser:
                raise RuntimeError("PSUM bank %s re-allocated while still live" % t.us[0].name)
        deps = []
        for t in reads:
            for u in t.us:
                if u.w is not None:
                    deps.append(u.w)
                if u.excl:
                    deps.extend(u.rs)
        for t in writes:
            for u in t.us:
                if u.w is not None:
                    deps.append(u.w)
                deps.extend(u.rs)
        deps = [d for d in deps if d.blk == self.nblocks]
        op = Op(eng, len(self.all_ops), fn, deps, dma)
        op.blk = self.nblocks
        self.all_ops.append(op)
        self.pending[eng].append(op)
        for t in reads:
            for u in t.us:
                u.rs.append(op)
        for t in writes:
            for u in t.us:
                u.w = op
                u.rs = []
        if dma:
            s = self.dnext
            self.dnext = (self.dnext + 1) % N_DMA_SEMS
            if self.dlast[s] is not None and self.dlast[s].blk == self.nblocks:
                op.deps.append(self.dlast[s])
            inc = getattr(self, "_next_inc", 16)
            self.dcount[s] += inc
            op.dsem = s
            op.dval = self.dcount[s]
            op.inc = inc
            self.dlast[s] = op
        return op

    def collective(self, kind, groups, out, in_):
        self._next_inc = 1
        try:
            op = self.add("pool", lambda eng: eng.collective_compute(kind, ALU.bypass, replica_groups=groups,
                                                                      ins=[in_.ap.opt()], outs=[out.ap.opt()]),
                          [in_], [out], dma=True)
        finally:
            self._next_inc = 16
        return op

    def flush(self, barrier=True):
        nc = self.nc
        needed = set()
        for e in ENGS:
            for op in self.pending[e]:
                for d in op.deps:
                    if d.dma:
                        continue
                    if d.eng == op.eng and (d.eng == "pe" or not SAME_ENGINE_SYNC) and not op.dma:
                        continue
                    needed.add(d.idx)
        if barrier:
            for e in ENGS:
                if e != "sp" and self.pending[e]:
                    for op in reversed(self.pending[e]):
                        if not op.dma:
                            needed.add(op.idx)
                            break
        for e in ENGS:
            for op in self.pending[e]:
                if op.dma:
                    op.sig = (self.dsems[op.dsem], op.dval)
                elif op.idx in needed and op.sig is None:
                    self.ecount[e] += 1
                    op.sig = (self.esem[e], self.ecount[e])
        for e in ENGS:
            wd = self.waited[e]
            for op in self.pending[e]:
                ws = {}
                for d in op.deps:
                    if not d.dma and d.eng == op.eng and (d.eng == "pe" or not SAME_ENGINE_SYNC) and not op.dma:
                        continue
                    if d.sig is None:
                        raise RuntimeError("dep without signal (emitted in an earlier block?)")
                    sem, val = d.sig
                    k = id(sem)
                    if wd.get(k, 0) >= val:
                        continue
                    if k not in ws or ws[k][1] < val:
                        ws[k] = (sem, val)
                for k, (sem, val) in ws.items():
                    wd[k] = val
                op.waits = list(ws.values())
        finals = []
        if barrier:
            for e in ENGS:
                if e != "sp" and self.ecount[e] > 0:
                    finals.append((self.esem[e], self.ecount[e]))
            for s in range(N_DMA_SEMS):
                if self.dcount[s] > 0:
                    finals.append((self.dsems[s], self.dcount[s]))
        pend = self.pending
        self.pending = {e: [] for e in ENGS}
        waited = self.waited

        def run(eng_name, eng):
            for op in pend[eng_name]:
                for sem, val in op.waits:
                    eng.wait_ge(sem, val)
                ins = op.fn(eng)
                if op.sig is not None:
                    if op.dma:
                        ins.then_inc(op.sig[0], op.inc)
                    else:
                        ins.then_inc(op.sig[0], 1)
            if barrier:
                wd = waited[eng_name]
                for sem, val in finals:
                    if wd.get(id(sem), 0) < val:
                        eng.wait_ge(sem, val)
                        wd[id(sem)] = val

        with nc.Block() as block:
            @block.sync
            def _(eng):
                run("sp", eng)

            @block.tensor
            def _(eng):
                run("pe", eng)

            @block.scalar
            def _(eng):
                run("act", eng)

            @block.vector
            def _(eng):
                run("dve", eng)

            @block.gpsimd
            def _(eng):
                run("pool", eng)
        self.nblocks += 1

    def dma(self, out, in_, q="sp"):
        reads = [in_] if isinstance(in_, T) else []
        writes = [out] if isinstance(out, T) else []
        o = out.ap if isinstance(out, T) else out
        i = in_.ap if isinstance(in_, T) else in_
        return self.add(q, lambda eng: eng.dma_start(out=o, in_=i), reads, writes, dma=True)

    def mm(self, out, lhsT, rhs, start=True, stop=True, extra_reads=()):
        return self.add("pe", lambda eng: eng.matmul(out.ap, lhsT.ap, rhs.ap, start=start, stop=stop),
                        [lhsT, rhs] + list(extra_reads), [out])

    def transpose(self, out, in_, ident):
        return self.add("pe", lambda eng: eng.transpose(out.ap, in_.ap, ident.ap), [in_, ident], [out])

    def act(self, out, in_, func, bias=None, scale=1.0, accum=None, eng="act"):
        reads = [in_]
        kw = {}
        if isinstance(bias, T):
            reads.append(bias)
            kw["bias"] = bias.ap
        elif bias is not None:
            kw["bias"] = bias
        if isinstance(scale, T):
            reads.append(scale)
            kw["scale"] = scale.ap
        else:
            kw["scale"] = scale
        writes = [out]
        if accum is not None:
            kw["accum_out"] = accum.ap
            writes.append(accum)
        return self.add(eng, lambda e: e.activation(out.ap, in_.ap, func, **kw), reads, writes)

    def tt(self, out, a, b, op, eng="dve"):
        return self.add(eng, lambda e: e.tensor_tensor(out.ap, a.ap, b.ap, op), [a, b], [out])

    def ts(self, out, a, s1, op0, s2=None, op1=None, eng="dve", accum=None):
        reads = [a]
        v1 = s1
        if isinstance(s1, T):
            reads.append(s1)
            v1 = s1.ap
        v2 = s2
        if isinstance(s2, T):
            reads.append(s2)
            v2 = s2.ap
        writes = [out]
        kw = {}
        if op1 is not None:
            kw["op1"] = op1
        if accum is not None:
            kw["accum_out"] = accum.ap
            writes.append(accum)
        return self.add(eng, lambda e: e.tensor_scalar(out.ap, a.ap, v1, v2, op0, **kw), reads, writes)

    def stt(self, out, a, s, b, op0, op1, eng="dve"):
        reads = [a, b]
        v = s
        if isinstance(s, T):
            reads.append(s)
            v = s.ap
        return self.add(eng, lambda e: e.scalar_tensor_tensor(out.ap, a.ap, v, b.ap, op0, op1), reads, [out])

    def copy(self, out, in_, eng="dve"):
        if eng == "act":
            return self.add("act", lambda e: e.copy(out.ap, in_.ap), [in_], [out])
        return self.add(eng, lambda e: e.tensor_copy(out.ap, in_.ap), [in_], [out])

    def memset(self, out, val, eng="dve"):
        return self.add(eng, lambda e: e.memset(out.ap, val), [], [out])

    def recip(self, out, in_):
        return self.add("dve", lambda e: e.reciprocal(out.ap, in_.ap), [in_], [out])


DBG = {}
D = 1024
KC = 8
DFF = 2816
NJ = DFF // 128
EPS = 1e-6
NTOK = 2048
NCTX = 256


def make_consts():
    i = np.arange(128)
    s = i[:, None]
    l = i[None, :]
    same = (s // 64) == (l // 64)
    f = lambda a: np.asarray(a, np.float32)
    blocks = [
        ("ident", f(np.eye(128))),
        ("ones", f(np.ones((128, 128)))),
        ("LE", f(s <= l)), ("GE", f(s >= l)), ("GT", f(s > l)), ("LT", f(s < l)),
        ("LE64", f((s <= l) & same)), ("GE64", f((s >= l) & same)),
        ("nGT64", f((s > l) & same) * (-1.0 / 16)), ("nLT64", f((s < l) & same) * (-1.0 / 16)),
        ("nLE64", f((s <= l) & same) * (-1.0 / 16)), ("nGE64", f((s >= l) & same) * (-1.0 / 16)),
        ("nCH", f((s // 64) == np.arange(2)[None, :]) * (-1.0 / 16)),
    ]
    off = {}
    o = 0
    for n, a in blocks:
        off[n] = (o, a.shape[1])
        o += a.shape[1]
    return np.concatenate([a for _, a in blocks], axis=1), off


CST, CST_OFF = make_consts()
NCST = CST.shape[1]


class Buf:
    def __init__(self, ap, name, ntok, gran):
        self.ap = ap
        self.gran = gran
        self.units = [U("%s_%d" % (name, i)) for i in range((ntok + gran - 1) // gran)]

    def tok(self, a, b, sel=None):
        us = self.units[a // self.gran:(b - 1) // self.gran + 1]
        ap = self.ap
        if sel is None:
            return T(ap[..., a:b] if False else _lastslice(ap, a, b), us)
        return T(_lastslice(ap[sel], a, b), us)

    def all(self):
        return T(self.ap, self.units)


def _lastslice(ap, a, b):
    nd = len(ap.shape)
    key = tuple([slice(None)] * (nd - 1) + [slice(a, b)])
    return ap[key]


class Ctx:
    def __init__(self, nc, P, st):
        self.nc = nc
        self.P = P
        self.banks = [P.ps(st, "bank%d" % i, [128, 512], F32) for i in range(8)]
        self.bi = 0
        self.cst = P.sb(st, "cst_sb", [128, NCST], F32)
        self.cstb = P.sb(st, "cst_sbb", [128, NCST], BF16)

    def bank(self):
        b = self.banks[self.bi]
        self.bi = (self.bi + 1) % 8
        b.us[0].ser += 1
        return T(b.ap, b.us, b.us[0].ser)

    def c(self, name, bf=False, rows=128):
        o, w = CST_OFF[name]
        t = self.cstb if bf else self.cst
        return t[0:rows, o:o + w]


def sb_alloc(P, st, name, shape, dtype):
    return P.sb(st, name, shape, dtype)


def emit_mod(cx, st, cvec_d, modw_d, modb_d, modv):
    P = cx.P
    with contextlib.ExitStack() as s2:
        cv = P.sb(s2, "cv", [128, 8, 2], F32)
        scv = P.sb(s2, "scv", [128, 8, 2], BF16)
        mb = P.sb(s2, "mb", [128, 48], F32)
        wb = [P.sb(s2, "mw%d" % i, [128, 8, 512], BF16) for i in range(3)]
        P.dma(cv, cvec_d)
        P.dma(mb, modb_d)
        P.act(scv, cv, AF.Silu)
        ps = cx.bank()
        for g in range(12):
            w = wb[g % 3]
            P.dma(w, modw_d[:, :, g * 512:(g + 1) * 512], q="pool")
            for jj in range(4):
                jc = g * 4 + jj
                for k in range(8):
                    P.mm(ps[:, jc * 2:jc * 2 + 2], w[:, k, jj * 128:(jj + 1) * 128], scv[:, k, :],
                         start=(k == 0), stop=(k == 7))
        psv = ps.v(ps.ap[:, 0:96].rearrange("p (j t) -> p j t", t=2))
        mbv = mb.v(mb.ap.unsqueeze(2).to_broadcast([128, 48, 2]))
        P.tt(modv, psv, mbv, ALU.add)
        P.flush()


def emit_modparams(cx, st, modv, ng_sb, name):
    P = cx.P
    out = {}
    for col, tag in ((0, "x"), (1, "c")):
        for nm, (jscale, jshift, jgate, gi) in (("1", (1, 0, 2, 0)), ("2", (4, 3, 5, 1))):
            G = P.sb(st, "%sG%s%s" % (name, nm, tag), [128, 8], F32)
            S = P.sb(st, "%sS%s%s" % (name, nm, tag), [128, 8], F32)
            g = P.sb(st, "%sg%s%s" % (name, nm, tag), [128, 8], F32)
            sc = modv.v(modv.ap[:, jscale * 8:(jscale + 1) * 8, col])
            sh = modv.v(modv.ap[:, jshift * 8:(jshift + 1) * 8, col])
            ga = modv.v(modv.ap[:, jgate * 8:(jgate + 1) * 8, col])
            P.stt(G, sc, 1.0, ng_sb.v(ng_sb.ap[:, gi, :]), ALU.add, ALU.mult)
            P.copy(S, sh)
            P.copy(g, ga)
            out["G" + nm + tag] = G
            out["S" + nm + tag] = S
            out["g" + nm + tag] = g
    return out


class NormScratch:
    def __init__(self, P, st, N):
        self.N = N
        NormScratch.cnt = getattr(NormScratch, "cnt", 0) + 1
        tg = "n%d_" % NormScratch.cnt
        self.sq = [P.sb(st, tg + "sq%d" % i, [128, N], BF16) for i in range(2)]
        self.rstd = P.sb(st, tg + "rstd", [128, N], F32)
        self.tmp = [P.sb(st, tg + "tmp%d" % i, [128, N], F32) for i in range(2)]


def emit_normmod(cx, ns, src, n, G, S, dst):
    P = cx.P
    ps = cx.bank()
    for c in range(8):
        sq = ns.sq[c % 2][:, 0:n]
        P.act(sq, src(c), AF.Square)
        P.mm(ps[:, 0:n], cx.c("ones", bf=True), sq, start=(c == 0), stop=(c == 7))
    rs = ns.rstd[:, 0:n]
    P.act(rs, ps[:, 0:n], AF.Sqrt, bias=EPS, scale=1.0 / D)
    P.recip(rs, rs)
    for c in range(8):
        tmp = ns.tmp[c % 2][:, 0:n]
        P.stt(tmp, src(c), G[:, c:c + 1], rs, ALU.mult, ALU.mult)
        P.act(dst(c), tmp, AF.Identity, bias=S[:, c:c + 1], scale=1.0)


def emit_ffn(cx, st, f_in_d, f_out_d, segs, ns):
    P = cx.P
    NT = sum(s[1] for s in segs)
    with contextlib.ExitStack() as s2:
        ns = NormScratch(P, s2, 512)
        hT = P.sb(s2, "f_hT", [128, 8, NT], BF16)
        hU = [U("f_hT%d" % i) for i in range(len(segs))]
        act = P.sb(s2, "f_act", [128, NJ, NT], BF16)
        aU = [U("f_act%d" % i) for i in range(len(segs))]
        NWIN = 5
        win = [P.sb(s2, "f_win%d" % i, [128, 8, 256], BF16) for i in range(NWIN)]
        wout = [P.sb(s2, "f_wout%d" % i, [128, NJ, 128], BF16) for i in range(3)]
        sg = [P.sb(s2, "f_sg%d" % i, [128, 512], F32) for i in range(2)]
        offs = []
        o = 0
        for si, (res, n, G, S, gate) in enumerate(segs):
            offs.append(o)
            emit_normmod(cx, ns, res, n, G, S,
                         lambda c, o=o, n=n, si=si: T(hT.ap[:, c, o:o + n], [hU[si]]))
            o += n
        for j in range(NJ):
            w = win[j % NWIN]
            P.dma(w, f_in_d[j], q="pool")
            for si, (res, n, G, S, gate) in enumerate(segs):
                o = offs[si]
                pg = cx.bank()
                pu = cx.bank()
                for k in range(8):
                    P.mm(pg[:, 0:n], w[:, k, 0:128], T(hT.ap[:, k, o:o + n], [hU[si]]), start=(k == 0), stop=(k == 7))
                for k in range(8):
                    P.mm(pu[:, 0:n], w[:, k, 128:256], T(hT.ap[:, k, o:o + n], [hU[si]]), start=(k == 0), stop=(k == 7))
                s_ = sg[(j * len(segs) + si) % 2][:, 0:n]
                P.act(s_, pg[:, 0:n], AF.Silu)
                P.tt(T(act.ap[:, j, o:o + n], [aU[si]]), s_, pu[:, 0:n], ALU.mult)
        for c in range(8):
            w = wout[c % 3]
            P.dma(w, f_out_d[c], q="pool")
            for si, (res, n, G, S, gate) in enumerate(segs):
                o = offs[si]
                py = cx.bank()
                for j in range(NJ):
                    P.mm(py[:, 0:n], w[:, j, :], T(act.ap[:, j, o:o + n], [aU[si]]), start=(j == 0), stop=(j == NJ - 1))
                r = res(c)
                P.stt(r, py[:, 0:n], gate[:, c:c + 1], r, ALU.mult, ALU.add)
        P.flush()


W0_Q, W0_K, W0_R, W0_U, W0_LR, W0_KT, W0_VT, W0_GT, W0_END = 0, 256, 512, 1024, 1536, 1568, 1824, 2336, 2848


def emit_l0_mixer(cx, st, d, xres, cres, mp, ns):
    P = cx.P
    with contextlib.ExitStack() as s2:
        sb = lambda name, shape, dt=F32: P.sb(s2, name, shape, dt)
        win = sb("a_win", [128, 8, W0_END], BF16)
        wout = sb("a_wout", [128, 8, 1024], BF16)
        gate = sb("a_gate", [33, 512], BF16)
        wsp = sb("a_ws", [128, 4, 128], BF16)
        sbi = sb("a_sb", [1, 512], BF16)
        gng = sb("a_gng", [128, 4])
        vng = sb("a_vng", [128, 512])
        for k in range(8):
            P.dma(win[:, k, :], d["w_in"][:, k, :], q="pool")
        for k in range(0, 8, 2):
            P.dma(wout[:, k:k + 2, :], d["w_out"][:, k:k + 2, :], q="pool")
        P.dma(gate, d["gate"], q="pool")
        P.dma(wsp, d["ws"], q="pool")
        P.dma(sbi, d["sb"], q="pool")
        P.dma(gng, d["gng"])
        P.dma(vng, d["vng"])
        Sf = [sb("a_Sf%d" % p, [128, 128]) for p in range(2)]
        Sb = [sb("a_Sb%d" % p, [128, 128]) for p in range(2)]
        NTL = 18
        Sbin = P.sb(s2, "a_Sbin", [128, NTL * 4, 128], BF16)
        SbinU = [U("Sbin%d" % i) for i in range(NTL)]
        sbin = lambda t, j, p: T(Sbin.ap[:, t * 4 + j * 2 + p, :], [SbinU[t]])
        Sfbf = [[sb("a_Sfbf%d%d" % (j, p), [128, 128], BF16) for p in range(2)] for j in range(2)]
        ns = NormScratch(P, s2, 128)
        xin = sb("a_xin", [128, 8, 128])
        hT = sb("a_hT", [128, 8, 128], BF16)
        lr1 = sb("a_lr1", [33, 128], BF16)
        v_tok = sb("a_vtok", [128, 512], BF16)
        e_t = sb("a_e", [128, 512])
        la = sb("a_la", [128, 512])
        expc = sb("a_expc", [128, 512])
        kwf = sb("a_kwf", [128, 256], BF16)
        kwb = sb("a_kwb", [128, 256], BF16)
        Ep = sb("a_Ep", [128, 512])
        Em = sb("a_Em", [128, 512])
        qd = sb("a_qd", [128, 4, 128], BF16)
        kd = sb("a_kd", [128, 4, 128], BF16)
        dec = sb("a_dec", [128, 4, 2])
        Pf = sb("a_Pf", [128, 4, 128], BF16)
        Pb = sb("a_Pb", [128, 4, 128], BF16)
        osq = sb("a_osq", [128, 512])
        orstd = sb("a_orstd", [128, 512])
        a1 = e_t
        rs = expc
        gel = Em
        vnb = sb("a_vnb", [128, 512], BF16)
        abT = sb("a_abT", [128, 8, 128], BF16)
        st8 = sb("a_st8", [128, 8])
        P.memset(lr1, 1.0)
        for p in range(2):
            P.memset(Sf[p], 0.0)
            P.memset(Sb[p], 0.0)

        def front(src, G, S):
            emit_normmod(cx, ns, src, 128, G, S, lambda c: hT[:, c, :])

        def gates(lo, hi):
            ps_lr = cx.bank()
            for k in range(8):
                P.mm(ps_lr[0:32, 0:128], win[:, k, W0_LR:W0_LR + 32], hT[:, k, :], start=(k == 0), stop=(k == 7))
            P.copy(lr1[0:32, :], ps_lr[0:32, 0:128], eng="act")
            ps_z = cx.bank()
            P.mm(ps_z, lr1, gate)
            P.act(e_t[:, lo:hi], ps_z[:, lo:hi], AF.Exp, scale=-1.0)
            P.act(la[:, lo:hi], e_t[:, lo:hi], AF.Ln, bias=1.0)

        def proj_tok(c0, n):
            ps = cx.bank()
            for k in range(8):
                P.mm(ps[:, 0:n], hT[:, k, :], win[:, k, c0:c0 + n], start=(k == 0), stop=(k == 7))
            return ps

        def proj_fm(c0, nchunks):
            ps = cx.bank()
            for i in range(nchunks):
                for k in range(8):
                    P.mm(ps[:, i * 128:(i + 1) * 128], win[:, k, c0 + i * 128:c0 + (i + 1) * 128], hT[:, k, :],
                         start=(k == 0), stop=(k == 7))
            return ps

        def state_update(S, decT, slot, j, ps_d):
            for hh in range(2):
                r0 = hh * 64
                P.stt(S[r0:r0 + 64, :], S[r0:r0 + 64, :], decT[r0:r0 + 64, slot, j:j + 1],
                      ps_d[r0:r0 + 64, hh * 128:(hh + 1) * 128], ALU.mult, ALU.add)

        def pass1_tile(src, G, S, store_idx, rd):
            front(src, G, S)
            gates(rd * 256, rd * 256 + 256)
            lad = la[:, rd * 256:(rd + 1) * 256]
            ps_c = cx.bank()
            P.mm(ps_c[:, 0:256], cx.c("nLT64" if rd else "nGT64"), lad)
            P.act(expc[:, 0:256], ps_c[:, 0:256], AF.Exp)
            ps_k = proj_tok(W0_KT, 256)
            P.tt(kwf, ps_k[:, 0:256], expc[:, 0:256], ALU.mult)
            ps_v = proj_tok(W0_VT, 512)
            P.copy(v_tok, ps_v, eng="act")
            ps_t = cx.bank()
            for p in range(2):
                P.mm(ps_t[:, p * 2:p * 2 + 2], la[:, rd * 256 + p * 128:rd * 256 + (p + 1) * 128], cx.c("nCH"))
            decv = dec.v(dec.ap[:, 0:2, :])
            P.act(decv, ps_t.v(ps_t.ap[:, 0:4].rearrange("p (a b) -> p a b", b=2)), AF.Exp)
            for p in range(2):
                for j in ((1, 0) if rd else (0, 1)):
                    ps_d = cx.bank()
                    P.mm(ps_d[:, 0:256], kwf[64 * j:64 * j + 64, p * 128:(p + 1) * 128],
                         v_tok[64 * j:64 * j + 64, p * 256:(p + 1) * 256])
                    if store_idx is not None:
                        P.copy(sbin(store_idx, j, p), Sf[p], eng="act")
                    state_update(Sf[p], dec, p, j, ps_d)

        def pass2_tile(src, G, S, res, gatev, store_idx, rd):
            sd = 1 - rd
            front(src, G, S)
            gates(0, 512)
            ps_c = cx.bank()
            P.mm(ps_c[:, 0:256], cx.c("nGT64"), la[:, 0:256])
            P.mm(ps_c[:, 256:512], cx.c("nLT64"), la[:, 256:512])
            P.act(expc, ps_c, AF.Exp)
            ps_k = proj_tok(W0_KT, 256)
            P.tt(kwf, ps_k[:, 0:256], expc[:, rd * 256:(rd + 1) * 256], ALU.mult)
            ps_v = proj_tok(W0_VT, 512)
            P.copy(v_tok, ps_v, eng="act")
            ps_b = cx.bank()
            for p in range(2):
                for dr in range(2):
                    slot = p * 2 + dr
                    lsl = la[:, dr * 256 + p * 128:dr * 256 + (p + 1) * 128]
                    P.mm(ps_b[:, slot * 128:(slot + 1) * 128], lsl, cx.c("nLE64" if dr == 0 else "nGE64"))
            P.act(Ep, ps_b, AF.Exp)
            P.act(Em, ps_b, AF.Exp, scale=-1.0)
            ps_t = cx.bank()
            for p in range(2):
                for dr in range(2):
                    slot = p * 2 + dr
                    lsl = la[:, dr * 256 + p * 128:dr * 256 + (p + 1) * 128]
                    P.mm(ps_t[:, slot * 2:slot * 2 + 2], lsl, cx.c("nCH"))
            P.act(dec, ps_t.v(ps_t.ap[:, 0:8].rearrange("p (a b) -> p a b", b=2)), AF.Exp)
            ps_q = proj_fm(W0_Q, 2)
            for p in range(2):
                q3 = ps_q.v(ps_q.ap[:, p * 128:(p + 1) * 128].unsqueeze(1).to_broadcast([128, 2, 128]))
                Ep3 = Ep.v(Ep.ap[:, p * 256:(p + 1) * 256].rearrange("p (b t) -> p b t", b=2))
                P.stt(qd[:, 2 * p:2 * p + 2, :], q3, 0.125, Ep3, ALU.mult, ALU.mult)
            ps_kT = proj_fm(W0_K, 2)
            for p in range(2):
                k3 = ps_kT.v(ps_kT.ap[:, p * 128:(p + 1) * 128].unsqueeze(1).to_broadcast([128, 2, 128]))
                Em3 = Em.v(Em.ap[:, p * 256:(p + 1) * 256].rearrange("p (b t) -> p b t", b=2))
                P.tt(kd[:, 2 * p:2 * p + 2, :], k3, Em3, ALU.mult)
            pv = lambda t_, par: t_.v(t_.ap.rearrange("p (i two t) -> p two i t", two=2, t=128)[:, par])
            mF = cx.c("LE64")
            mB = cx.c("GE64")
            for dr in range(2):
                ps2 = [cx.bank(), cx.bank()]
                for h in range(4):
                    p, b0, par, i = h // 2, 64 * (h % 2), h % 2, h // 2
                    P.mm(ps2[par][:, i * 128:(i + 1) * 128], kd[b0:b0 + 64, p * 2 + dr, :], qd[b0:b0 + 64, p * 2 + dr, :])
                for par in range(2):
                    dst = (Pf if dr == 0 else Pb)[:, par * 2:par * 2 + 2, :]
                    msk = mF if dr == 0 else mB
                    P.tt(dst, ps2[par].v(ps2[par].ap[:, 0:256].rearrange("p (h t) -> p h t", h=2)),
                         msk.v(msk.ap.unsqueeze(1).to_broadcast([128, 2, 128])), ALU.mult)
            for p in range(2):
                for j in ((1, 0) if rd else (0, 1)):
                    P.copy(Sfbf[j][p], Sf[p], eng="act")
                    ps_d = cx.bank()
                    P.mm(ps_d[:, 0:256], kwf[64 * j:64 * j + 64, p * 128:(p + 1) * 128],
                         v_tok[64 * j:64 * j + 64, p * 256:(p + 1) * 256])
                    state_update(Sf[p], dec, p * 2 + rd, j, ps_d)
            ps_o2 = [cx.bank(), cx.bank()]
            for h in range(4):
                p, b0, par, i = h // 2, 64 * (h % 2), h % 2, h // 2
                c0 = i * 128
                po = ps_o2[par]
                P.mm(po[:, c0:c0 + 128], v_tok[:, h * 128:(h + 1) * 128], Pf[:, par * 2 + i, :], start=True, stop=False)
                P.mm(po[:, c0:c0 + 128], v_tok[:, h * 128:(h + 1) * 128], Pb[:, par * 2 + i, :], start=False, stop=False)
                for j in (0, 1):
                    P.mm(po[:, c0 + 64 * j:c0 + 64 * j + 64], Sfbf[j][p][b0:b0 + 64, :],
                         qd[b0:b0 + 64, p * 2 + rd, 64 * j:64 * j + 64], start=False, stop=False)
                    P.mm(po[:, c0 + 64 * j:c0 + 64 * j + 64], sbin(store_idx, j, p)[b0:b0 + 64, :],
                         qd[b0:b0 + 64, p * 2 + sd, 64 * j:64 * j + 64], start=False, stop=(j == 1))
            for par in range(2):
                P.act(osq[:, par * 256:(par + 1) * 256], ps_o2[par][:, 0:256], AF.Square)
            ps_n = cx.bank()
            P.mm(ps_n, cx.c("ones"), osq)
            P.act(orstd, ps_n, AF.Sqrt, bias=EPS, scale=1.0 / 128)
            P.recip(orstd, orstd)
            for par in range(2):
                P.tt(a1[:, par * 256:(par + 1) * 256], ps_o2[par][:, 0:256], orstd[:, par * 256:(par + 1) * 256], ALU.mult)
            ps_r = proj_fm(W0_R, 4)
            P.act(rs, ps_r, AF.Silu)
            for par in range(2):
                a1p = a1[:, par * 256:(par + 1) * 256]
                a1p3 = a1p.v(a1p.ap.rearrange("p (i t) -> p i t", i=2))
                gp = gng.v(gng.ap.rearrange("p (i two) -> p two i", two=2)[:, par].unsqueeze(2).to_broadcast([128, 2, 128]))
                P.tt(a1p3, a1p3, gp, ALU.mult)
                ab_par = abT.v(abT.ap[:, 0:4, :].rearrange("p (i two) t -> p two i t", two=2)[:, par])
                P.tt(ab_par, a1p3, pv(rs, par), ALU.mult)
            ps_g = proj_tok(W0_GT, 512)
            P.memset(st8, 0.0)
            P.act(gel, ps_g, AF.Gelu_apprx_tanh, accum=st8[:, 0:1])
            P.act(osq, gel, AF.Square, accum=st8[:, 1:2])
            P.ts(st8[:, 2:3], st8[:, 0:1], 1.0 / 512, ALU.mult)
            P.tt(st8[:, 3:4], st8[:, 2:3], st8[:, 2:3], ALU.mult)
            P.stt(st8[:, 4:5], st8[:, 1:2], 1.0 / 512, st8[:, 3:4], ALU.mult, ALU.subtract)
            P.act(st8[:, 5:6], st8[:, 4:5], AF.Sqrt, bias=EPS, scale=1.0)
            P.recip(st8[:, 5:6], st8[:, 5:6])
            P.stt(st8[:, 6:7], st8[:, 2:3], -1.0, st8[:, 5:6], ALU.mult, ALU.mult)
            P.act(osq, gel, AF.Identity, bias=st8[:, 6:7], scale=st8[:, 5:6])
            P.tt(vnb, osq, vng, ALU.mult)
            ps_u = proj_fm(W0_U, 4)
            P.act(gel, ps_u, AF.Gelu_apprx_tanh)
            ps_s = cx.bank()
            for g in range(4):
                P.mm(ps_s[:, g * 128:(g + 1) * 128], vnb[:, g * 128:(g + 1) * 128], wsp[:, g, :], start=True, stop=False)
                P.mm(ps_s[:, g * 128:(g + 1) * 128], cx.c("ones", bf=True, rows=1), sbi[:, g * 128:(g + 1) * 128],
                     start=False, stop=True)
            P.tt(abT.v(abT.ap[:, 4:8, :]), gel.v(gel.ap.rearrange("p (h t) -> p h t", h=4)),
                 ps_s.v(ps_s.ap.rearrange("p (h t) -> p h t", h=4)), ALU.mult)
            if DBG.get("dump") and store_idx == DBG.get("dump_tile", 2):
                dd = DBG["dbg_ap"]
                dt_ = sb("a_dbg", [128, 1024])
                P.copy(dt_.v(dt_.ap[:, 0:1024].rearrange("p (c t) -> p c t", c=8)), hT)
                P.dma(dd[:, 0:1024], dt_)
                P.copy(dt_[:, 0:512], la)
                P.copy(dt_[:, 512:1024], a1)
                P.dma(dd[:, 1024:2048], dt_)
                P.copy(dt_.v(dt_.ap[:, 0:1024].rearrange("p (c t) -> p c t", c=8)), abT)
                P.dma(dd[:, 2048:3072], dt_)
            for half in range(2):
                ps_y = cx.bank()
                for cc in range(4):
                    c = half * 4 + cc
                    for k in range(8):
                        P.mm(ps_y[:, cc * 128:(cc + 1) * 128], wout[:, k, c * 128:(c + 1) * 128], abT[:, k, :],
                             start=(k == 0), stop=(k == 7))
                for cc in range(4):
                    c = half * 4 + cc
                    r = res(c)
                    P.stt(r, ps_y[:, cc * 128:(cc + 1) * 128], gatev[:, c:c + 1], r, ALU.mult, ALU.add)

        for t in (0, 1):
            pass1_tile(lambda c, t=t: cres.tok(t * 128, (t + 1) * 128, sel=(slice(None), c)), mp["G1c"], mp["S1c"], t, 0)
        for t in range(16):
            pass1_tile(lambda c, t=t: xres.tok(t * 128, (t + 1) * 128, sel=(slice(None), c)), mp["G1x"], mp["S1x"], 2 + t, 0)
        nc = cx.nc
        gx_mine = nc.dram_tensor("gx_mine", [128, 256], F32, kind="Internal").ap()
        gx_gath = nc.dram_tensor("gx_gath", [256, 256], F32, kind="Internal").ap()
        gxm = T(gx_mine, U("d_gxm"))
        gxg = T(gx_gath, U("d_gxg"))
        for p in range(2):
            P.dma(gxm.v(gx_mine[:, p * 128:(p + 1) * 128]), Sf[p], q="pool")
        P.collective("AllGather", [[0, 1], [2, 3], [4, 5], [6, 7]], gxg, gxm)
        rcv = sb("a_rcv", [128, 2, 256])
        pmk = sb("a_pmk", [128, 2])
        P.dma(pmk, d["pmask"])
        P.dma(rcv, gxg.v(gx_gath.rearrange("(r q) n -> q r n", r=2)))
        P.ts(rcv[:, 0, :], rcv[:, 0, :], pmk[:, 0:1], ALU.mult)
        P.stt(rcv[:, 1, :], rcv[:, 1, :], pmk[:, 1:2], rcv[:, 0, :], ALU.mult, ALU.add)
        for p in range(2):
            P.memset(Sf[p], 0.0)
        for t in (1, 0):
            rv = lambda c, t=t: cres.tok(t * 128, (t + 1) * 128, sel=(slice(None), c))
            pass2_tile(rv, mp["G1c"], mp["S1c"], rv, mp["g1c"], t, 1)
        for p in range(2):
            P.copy(Sf[p], rcv[:, 1, p * 128:(p + 1) * 128])
        for t in range(DBG.get("np2", 16) - 1, -1, -1):
            rv = lambda c, t=t: xres.tok(t * 128, (t + 1) * 128, sel=(slice(None), c))
            pass2_tile(rv, mp["G1x"], mp["S1x"], rv, mp["g1x"], 2 + t, 1)
        P.flush()


def fm(a):
    a = np.asarray(a, np.float32)
    n = a.shape[0]
    return np.ascontiguousarray(a.T.reshape(8, 128, n).transpose(1, 0, 2))


def unfm(a):
    n = a.shape[2]
    return np.ascontiguousarray(a.transpose(1, 0, 2).reshape(1024, n).T)


def wk(w):
    w = np.asarray(w, np.float32)
    kc = w.shape[0] // 128
    return np.ascontiguousarray(w.reshape(kc, 128, w.shape[1]).transpose(1, 0, 2))


def vec_fm(v):
    v = np.asarray(v, np.float32)
    return np.ascontiguousarray(v.reshape(-1, 128).T)


def prep_common(inp, layer, b):
    d = {}
    d["cvec"] = np.ascontiguousarray(np.stack([vec_fm(inp["c"][b]), vec_fm(inp["c_ctx"])], axis=2))
    d["modw"] = wk(inp["mod_w"][layer])
    d["modb"] = vec_fm(inp["mod_b"][layer])
    d["ng"] = np.ascontiguousarray(np.stack([vec_fm(inp["norm_g"][layer, 0]), vec_fm(inp["norm_g"][layer, 1])], axis=1))
    fw_in = np.asarray(inp["ffn_w_in"][layer], np.float32)
    gu = np.stack([fw_in[:, :DFF].reshape(1024, NJ, 128), fw_in[:, DFF:].reshape(1024, NJ, 128)], axis=2)
    gu = gu.reshape(8, 128, NJ, 256).transpose(2, 1, 0, 3)
    d["f_in"] = np.ascontiguousarray(gu)
    fo = np.asarray(inp["ffn_w_out"][layer], np.float32).reshape(NJ, 128, 8, 128).transpose(2, 1, 0, 3)
    d["f_out"] = np.ascontiguousarray(fo)
    d["cst"] = CST
    return d


def prep_l0(inp, b, half):
    d = prep_common(inp, 0, b)
    x = np.asarray(inp["x"][b], np.float32)
    ctx = np.asarray(inp["ctx"][b], np.float32)
    own = x[half * NTOK:(half + 1) * NTOK]
    oth = x[(1 - half) * NTOK:(2 - half) * NTOK]
    if half == 1:
        own, oth, ctx = own[::-1], oth[::-1], ctx[::-1]
    d["xo"], d["xx"], d["xc"] = fm(own), fm(oth), fm(ctx)
    w = np.asarray(inp["ab_w_in"][0], np.float32)
    k, v, lrf, lrb, q, r, u, g = np.split(w, np.cumsum([256, 512, 16, 16, 256, 512, 512])[:-1].tolist() + [2080], axis=1)
    gw = np.asarray(inp["ab_gate_w"][0], np.float32)
    gb = np.asarray(inp["ab_gate_b"][0], np.float32)
    if half == 1:
        lrf, lrb = lrb, lrf
        gw, gb = gw[::-1], gb[::-1]
    wcat = np.concatenate([q, k, r, u, lrf, lrb, k, v, g], axis=1)
    assert wcat.shape[1] == W0_END
    d["w_in"] = wk(wcat)
    gt = np.zeros((33, 512), np.float32)
    gt[0:16, 0:256] = gw[0]
    gt[16:32, 256:512] = gw[1]
    gt[32, 0:256] = gb[0]
    gt[32, 256:512] = gb[1]
    d["gate"] = gt
    d["gng"] = vec_fm(inp["ab_gla_norm_g"][0])
    d["vng"] = np.ascontiguousarray(np.broadcast_to(np.asarray(inp["ab_vnorm_g"][0], np.float32)[None, :], (128, 512)))
    sw = np.asarray(inp["ab_spatial_w"][0], np.float32)
    sbv = np.asarray(inp["ab_spatial_b"][0], np.float32)
    if half == 1:
        sw = sw[:, ::-1, ::-1]
        sbv = sbv[:, ::-1]
    d["ws"] = np.ascontiguousarray(sw.transpose(2, 0, 1))
    d["sb"] = np.ascontiguousarray(sbv.reshape(1, 512))
    d["w_out"] = wk(inp["ab_w_out"][0])
    return d


L0_SHAPES = dict(xo=[128, 8, NTOK], xx=[128, 8, NTOK], xc=[128, 8, NCTX], cvec=[128, 8, 2], modw=[128, 8, 6144],
                 modb=[128, 48], ng=[128, 2, 8], w_in=[128, 8, W0_END], gate=[33, 512], gng=[128, 4], vng=[128, 512],
                 ws=[128, 4, 128], sb=[1, 512], w_out=[128, 8, 1024], f_in=[NJ, 128, 8, 256], f_out=[8, 128, NJ, 128],
                 cst=[128, NCST])


def load_consts(cx):
    P = cx.P
    P.dma(cx.cst, cx.cst_d)
    P.dma(cx.cstb, cx.cst_d, q="pool")


def l0_body(cx, s0, d, stop_after=None):
    P = cx.P
    xr = P.sb(s0, "xres", [128, 8, NTOK], F32)
    xres = Buf(xr.ap, "xres", NTOK, 128)
    cr = P.sb(s0, "cres", [128, 8, NCTX], F32)
    cres = Buf(cr.ap, "cres", NCTX, 128)
    for g in range(4):
        P.dma(xres.tok(g * 512, (g + 1) * 512), d["xo"][:, :, g * 512:(g + 1) * 512])
    P.dma(cres.all(), d["xc"])
    modv = P.sb(s0, "modv", [128, 48, 2], F32)
    ngs = P.sb(s0, "ngs", [128, 2, 8], F32)
    P.dma(ngs, d["ng"])
    emit_mod(cx, s0, d["cvec"], d["modw"], d["modb"], modv)
    mp = emit_modparams(cx, s0, modv, ngs, "l0")
    if stop_after != "mod":
        emit_l0_mixer(cx, s0, d, xres, cres, mp, None)
    if stop_after not in ("mod", "mixer"):
        for sgi in range(2):
            segs = []
            for g in range(2):
                a = sgi * 1024 + g * 512
                segs.append((lambda c, a=a: xres.tok(a, a + 512, sel=(slice(None), c)), 512,
                             mp["G2x"], mp["S2x"], mp["g2x"]))
            a = sgi * 128
            segs.append((lambda c, a=a: cres.tok(a, a + 128, sel=(slice(None), c)), 128,
                         mp["G2c"], mp["S2c"], mp["g2c"]))
            emit_ffn(cx, s0, d["f_in"], d["f_out"], segs, None)
    return xres, cres


def build_l0(stop_after=None):
    nc = bass.Bass("TRN2", target_bir_lowering=False)
    d = {k: nc.dram_tensor(k, list(s), F32, kind="ExternalInput").ap() for k, s in L0_SHAPES.items()}
    x1 = nc.dram_tensor("x1", [128, 8, NTOK], F32, kind="ExternalOutput").ap()
    c1 = nc.dram_tensor("c1", [128, 8, NCTX], F32, kind="ExternalOutput").ap()
    if DBG.get("dump"):
        DBG["dbg_ap"] = nc.dram_tensor("dbg", [128, 3072], F32, kind="ExternalOutput").ap()
    with contextlib.ExitStack() as st:
        P = Prog(nc, st)
        cx = Ctx(nc, P, st)
        cx.cst_d = d["cst"]
        load_consts(cx)
        xres, cres = l0_body(cx, st, d, stop_after)
        for g in range(4):
            P.dma(x1[:, :, g * 512:(g + 1) * 512], xres.tok(g * 512, (g + 1) * 512))
        P.dma(c1, cres.all())
        P.flush()
    return nc


W1_X, W1_B, W1_C, W1_DT, W1_Z, W1_END = 0, 2048, 2560, 3072, 3136, 5184
NSEQ = 2 * NTOK + 4
NCTXP = NCTX + 4

L1_SHAPES = dict(seq=[128, 8, NSEQ], xc1=[128, 8, NCTXP], cvec=[128, 8, 2], modw=[128, 8, 6144], modb=[128, 48],
                 ng=[128, 2, 8], w_in=[128, 8, W1_END], convw=[128, 24, 5], convb=[128, 24], dtb=[128, 64],
                 alog=[128, 64], dsk=[128, 32], ng1=[128, 16], w_out=[128, 16, 1024], fng=[128, 8],
                 f_in=[NJ, 128, 8, 256], f_out=[8, 128, NJ, 128], cst=[128, NCST])


def prep_l1(inp, x1b, ctx1b, b, half):
    d = prep_common(inp, 1, b)
    if x1b is not None:
        xcm = np.asarray(x1b, np.float32).reshape(64, 64, 1024).transpose(1, 0, 2).reshape(4096, 1024)
        own = xcm[half * NTOK:(half + 1) * NTOK]
        oth = xcm[(1 - half) * NTOK:(2 - half) * NTOK]
        ctx = np.asarray(ctx1b, np.float32)
        if half == 1:
            own, oth, ctx = own[::-1], oth[::-1], ctx[::-1]
        z2 = np.zeros((2, 1024), np.float32)
        d["seq"] = fm(np.concatenate([z2, own, oth, z2], axis=0))
        d["xc1"] = fm(np.concatenate([z2, ctx, z2], axis=0))
    w = np.asarray(inp["ssd_w_in"][0], np.float32)
    cw = np.asarray(inp["ssd_conv_w"][0], np.float32)
    dtb = np.asarray(inp["ssd_dt_bias"][0], np.float32)
    alog = np.asarray(inp["ssd_a_log"][0], np.float32)
    if half == 1:
        w = np.concatenate([w[:, :W1_DT], w[:, W1_DT + 32:W1_DT + 64], w[:, W1_DT:W1_DT + 32], w[:, W1_Z:]], axis=1)
        cw = cw[::-1]
        dtb = dtb[::-1]
        alog = alog[::-1]
    d["w_in"] = wk(w)
    d["convw"] = np.ascontiguousarray(cw.T.reshape(24, 128, 5).transpose(1, 0, 2))
    d["convb"] = vec_fm(inp["ssd_conv_b"][0])
    d["dtb"] = np.ascontiguousarray(np.broadcast_to(dtb.reshape(1, 64), (128, 64)))
    d["alog"] = np.ascontiguousarray(np.broadcast_to(alog.reshape(1, 64), (128, 64)))
    d["dsk"] = np.ascontiguousarray(np.broadcast_to(np.asarray(inp["ssd_d"][0], np.float32).reshape(1, 32), (128, 32)))
    d["ng1"] = vec_fm(inp["ssd_norm_g"][0])
    d["w_out"] = wk(inp["ssd_w_out"][0])
    d["fng"] = vec_fm(inp["final_norm_g"])
    return d


def emit_l1_ssd(cx, st, d, mp, sscr, yscr):
    xcache = cx.nc.dram_tensor("xcache", [16, 128, 2048], BF16, kind="Internal").ap()
    bcache = cx.nc.dram_tensor("bcache", [16, 128, 1024], BF16, kind="Internal").ap()
    sx_mine = cx.nc.dram_tensor("sx_mine", [128, 2048], F32, kind="Internal").ap()
    sx_gath = cx.nc.dram_tensor("sx_gath", [256, 2048], F32, kind="Internal").ap()
    return _emit_l1_ssd(cx, st, d, mp, sscr, yscr, xcache, bcache, sx_mine, sx_gath, d["pmask"])


def _emit_l1_ssd(cx, st, d, mp, sscr, yscr, xcache, bcache, sx_mine, sx_gath, pmask_d):
    P = cx.P
    with contextlib.ExitStack() as s2:
        sb = lambda name, shape, dt=F32: P.sb(s2, name, shape, dt)
        NCOL = W1_Z
        win = sb("s_win", [128, 8, NCOL], BF16)
        for k in range(8):
            P.dma(win[:, k, :], d["w_in"][:, k, 0:NCOL], q="pool")
        convw = sb("s_convw", [128, 24, 5])
        convb = sb("s_convb", [128, 24])
        dtb = sb("s_dtb", [128, 64])
        Aneg = sb("s_Aneg", [128, 64])
        dsk = sb("s_dsk", [128, 32])
        P.dma(convw, d["convw"])
        P.dma(convb, d["convb"])
        P.dma(dtb, d["dtb"])
        P.dma(Aneg, d["alog"])
        P.dma(dsk, d["dsk"])
        P.act(Aneg, Aneg, AF.Exp)
        P.ts(Aneg, Aneg, -1.0, ALU.mult)
        diagW = sb("s_diag", [128, 120, 128], BF16)
        identb = cx.c("ident", bf=True)
        for cc in range(24):
            for j in range(5):
                P.ts(diagW[:, cc * 5 + j, :], identb, convw[:, cc, j:j + 1], ALU.mult)
        S = sb("s_S", [128, 4, 512])
        SU = [U("s_S%d" % g) for g in range(4)]
        Sg_ = lambda g: T(S.ap[:, g, :], [SU[g]])
        Sbf = sb("s_Sbf", [128, 4, 512], BF16)
        SbfU = [U("s_Sbf%d" % g) for g in range(4)]
        Sbfg = lambda g: T(Sbf.ap[:, g, :], [SbfU[g]])
        small = [dict(dtv=sb("s_dt%d" % i, [128, 64]), dtA=sb("s_dtA%d" % i, [128, 64]), e=sb("s_e%d" % i, [128, 64]),
                      w=sb("s_w%d" % i, [128, 64]), dec=sb("s_dec%d" % i, [128, 64]), dtw=sb("s_dtw%d" % i, [128, 64]),
                      ahi=sb("s_ahi%d" % i, [128, 64], BF16), alo=sb("s_alo%d" % i, [128, 64], BF16),
                      atmp=sb("s_atmp%d" % i, [128, 64]))
                 for i in range(2)]
        preU = [U("s_pre%d" % i) for i in range(24)]
        cnt = {"tile": 0, "grp": 0}

        def bc_heads(t_, d0):
            return t_.v(t_.ap[:, d0:d0 + 8].unsqueeze(2).to_broadcast([128, 8, 64]))

        h8 = lambda t_: t_.v(t_.ap.rearrange("p (h q) -> p h q", h=8))

        def tile_front(src_d, pos, G, Sm, zero_left, zero_right, chunks, save_tile=None, load_tile=None):
            par = cnt["tile"] % 2
            cnt["tile"] += 1
            xw_ = xw[par]
            P.dma(xw_, src_d[:, :, pos * 128:pos * 128 + 132])
            emit_normmod(cx, ns, lambda c: xw_[:, c, :], 132, G, Sm, lambda c: hTw[:, c, :])
            if zero_left:
                P.memset(hTw[:, :, 0:2], 0.0)
            if zero_right:
                P.memset(hTw[:, :, 130:132], 0.0)
            sm = small[par]
            ps = cx.bank()
            for k in range(8):
                P.mm(ps[:, 0:64], hTw[:, k, 2:130], win[:, k, W1_DT:W1_DT + 64], start=(k == 0), stop=(k == 7))
            P.tt(sm["dtv"], ps[:, 0:64], dtb, ALU.add)
            P.act(sm["dtv"], sm["dtv"], AF.Exp)
            P.act(sm["dtv"], sm["dtv"], AF.Ln, bias=1.0)
            P.tt(sm["dtA"], sm["dtv"], Aneg, ALU.mult)
            P.copy(sm["ahi"], sm["dtA"])
            P.tt(sm["atmp"], sm["dtA"], sm["ahi"], ALU.subtract)
            P.copy(sm["alo"], sm["atmp"])
            ps2 = cx.bank()
            dtA = sm["dtA"]
            P.mm(ps2[:, 0:32], cx.c("LE"), dtA[:, 0:32])
            P.mm(ps2[:, 32:64], cx.c("GE"), dtA[:, 32:64])
            P.mm(ps2[:, 64:96], cx.c("GT"), dtA[:, 0:32])
            P.mm(ps2[:, 96:128], cx.c("LT"), dtA[:, 32:64])
            P.mm(ps2[:, 128:192], cx.c("ones"), dtA)
            P.act(sm["e"], ps2[:, 0:64], AF.Exp)
            P.act(sm["w"], ps2[:, 64:128], AF.Exp)
            P.act(sm["dec"], ps2[:, 128:192], AF.Exp)
            P.tt(sm["dtw"], sm["dtv"], sm["w"], ALU.mult)
            for cc in chunks:
                psp = cx.bank()
                for k in range(8):
                    P.mm(psp[:, 0:132], win[:, k, cc * 128:(cc + 1) * 128], hTw[:, k, :], start=(k == 0), stop=(k == 7))
                P.copy(T(pre.ap[:, cc, :], [preU[cc]]), psp[:, 0:132], eng="act")
            cT_ = cT[par]
            ct = lambda cc: T(cT_.ap[:, cc, :], [cTU[par][cc]])
            for cc in chunks:
                psc = cx.bank()
                for j in range(5):
                    P.mm(psc[:, 0:128], diagW[:, cc * 5 + j, :], T(pre.ap[:, cc, j:j + 128], [preU[cc]]),
                         start=(j == 0), stop=(j == 4))
                P.act(ct(cc), psc[:, 0:128], AF.Silu, bias=convb[:, cc:cc + 1], scale=1.0)
            xt_ = x_tok[par]
            if load_tile is not None:
                P.dma(T(xt_.ap, xtU[par]), xcache[load_tile])
                P.dma(B_tok[par], bcache[load_tile, :, 0:512])
                P.dma(T(cT_.ap[:, 16:20, :], cTU[par][16:20]), bcache[load_tile, :, 512:1024].rearrange("p (g t) -> p g t", g=4))
            else:
                for g in range(4):
                    pst = cx.bank()
                    for i in range(4):
                        P.mm(pst[:, i * 128:(i + 1) * 128], ct(g * 4 + i), identb)
                    P.copy(T(xt_.ap[:, g * 512:(g + 1) * 512], [xtU[par][g]]), pst, eng=("dve" if g % 2 else "act"))
                pst = cx.bank()
                for g in range(4):
                    P.mm(pst[:, g * 128:(g + 1) * 128], ct(16 + g), identb)
                P.copy(B_tok[par], pst, eng="act")
                if save_tile is not None:
                    P.dma(xcache[save_tile], T(xt_.ap, xtU[par]), q="pool")
                    P.dma(bcache[save_tile, :, 0:512], B_tok[par], q="pool")
                    P.dma(bcache[save_tile, :, 512:1024].rearrange("p (g t) -> p g t", g=4), T(cT_.ap[:, 16:20, :], cTU[par][16:20]), q="pool")
            xt = lambda g: T(xt_.ap[:, g * 512:(g + 1) * 512], [xtU[par][g]])
            bt = lambda g: B_tok[par][:, g * 128:(g + 1) * 128]
            return sm, ct, xt, bt

        def state_step(g, dr, sm, xt, bt, store_tile, xdtw_ring=None):
            xdtw_ring = xdtw_ring or xdtw
            h0 = dr * 32 + g * 8
            xw_ = xdtw_ring[cnt["grp"] % 2]
            cnt["grp"] += 1
            P.tt(h8(xw_), h8(xt(g)), bc_heads(sm["dtw"], h0), ALU.mult, eng="pool")
            ps = cx.bank()
            P.mm(ps, bt(g), xw_)
            if store_tile is not None:
                P.copy(Sbfg(g), Sg_(g), eng="act")
                P.dma(sscr[store_tile, :, g, :], Sbfg(g), q="pool")
            P.tt(h8(Sg_(g)), h8(Sg_(g)), bc_heads(sm["dec"], h0), ALU.mult)
            P.tt(Sg_(g), Sg_(g), ps, ALU.add)

        def state_tile(src_d, pos, G, Sm, dr, zl, zr, store_tile):
            sm, ct, xt, bt = tile_front(src_d, pos, G, Sm, zl, zr, range(20), save_tile=store_tile)
            for g in range(4):
                state_step(g, dr, sm, xt, bt, store_tile)

        def full_tile(t):
            sm, ct, xt, bt = tile_front(d["seq"], t, mp["G1x"], mp["S1x"], t == 0, False, range(20, 24), load_tile=t)
            P.dma(Sbin, sscr[t])
            dtA, dtv, e_ = sm["dtA"], sm["dtv"], sm["e"]
            for g in range(4):
                gp = g % 2
                Gm_, MT_, xdt_, t1_, yg_ = Gm[gp], MT[gp], xdt[gp], t1[gp], yg[gp]
                psg = cx.bank()
                P.mm(psg[:, 0:128], ct(16 + g), ct(20 + g))
                P.tt(Gm_[:, 0, :], psg[:, 0:128], cx.c("LE"), ALU.mult)
                P.tt(Gm_[:, 1, :], psg[:, 0:128], cx.c("GE"), ALU.mult)
                combos = [(dr, hh) for dr in range(2) for hh in range(2)]
                for i, (dr, hh) in enumerate(combos):
                    h0 = dr * 32 + g * 8
                    tri = cx.c("LE" if dr == 0 else "GE", bf=True)
                    for part, nm in enumerate(("ahi", "alo")):
                        av = sm[nm]
                        P.tt(segrhs[i][:, part, :, :], tri.v(tri.ap.unsqueeze(1).to_broadcast([128, 4, 128])),
                             av.v(av.ap[:, h0 + hh * 4:h0 + hh * 4 + 4].unsqueeze(2).to_broadcast([128, 4, 128])), ALU.mult,
                             eng="pool")
                pss = []
                for i, (dr, hh) in enumerate(combos):
                    um = cx.c("GT" if dr == 0 else "LT", bf=True)
                    p_ = cx.bank()
                    for part in range(2):
                        sr = segrhs[i][:, part, :, :]
                        P.mm(p_, um, sr.v(sr.ap.rearrange("p h t -> p (h t)")), start=(part == 0), stop=(part == 1))
                    pss.append(p_)
                for i, (dr, hh) in enumerate(combos):
                    P.act(Dexp[i], pss[i].v(pss[i].ap.rearrange("p (h t) -> p h t", h=4)), AF.Exp)
                for i, (dr, hh) in enumerate(combos):
                    P.tt(MT_[:, dr * 8 + hh * 4:dr * 8 + hh * 4 + 4, :], Dexp[i],
                         Gm_.v(Gm_.ap[:, dr, :].unsqueeze(1).to_broadcast([128, 4, 128])), ALU.mult)
                for dr in range(2):
                    P.tt(h8(xdt_[:, dr, :]), h8(xt(g)), bc_heads(dtv, dr * 32 + g * 8), ALU.mult, eng="pool")
                psy = cx.bank()
                for hh in range(8):
                    P.mm(psy[:, hh * 64:(hh + 1) * 64], MT_[:, hh, :], xdt_[:, 0, hh * 64:(hh + 1) * 64], start=True, stop=False)
                    P.mm(psy[:, hh * 64:(hh + 1) * 64], MT_[:, 8 + hh, :], xdt_[:, 1, hh * 64:(hh + 1) * 64], start=False, stop=True)
                P.copy(Sbfg(g), Sg_(g), eng="act")
                pso = cx.bank()
                P.mm(pso, ct(20 + g), Sbfg(g))
                pso2 = cx.bank()
                P.mm(pso2, ct(20 + g), Sbin[:, g, :])
                P.tt(h8(t1_), h8(pso), bc_heads(e_, 32 + g * 8), ALU.mult)
                P.tt(yg_, psy, t1_, ALU.add)
                P.tt(h8(t1_), h8(pso2), bc_heads(e_, g * 8), ALU.mult)
                P.tt(yg_, yg_, t1_, ALU.add)
                P.tt(h8(t1_), h8(xt(g)), dsk.v(dsk.ap[:, g * 8:g * 8 + 8].unsqueeze(2).to_broadcast([128, 8, 64])), ALU.mult)
                P.tt(yg_, yg_, t1_, ALU.add)
                P.dma(yscr[t, :, g * 512:(g + 1) * 512], yg_, q="pool")
                state_step(g, 1, sm, xt, bt, None)

        P.memset(T(S.ap, SU), 0.0)
        with contextlib.ExitStack() as sA:
            sba = lambda name, shape, dt=F32: P.sb(sA, name, shape, dt)
            nsA = NormScratch(P, sA, 260)
            xwA = [sba("A_xw%d" % i, [128, 8, 260]) for i in range(2)]
            hTwA = sba("A_hTw", [128, 8, 260], BF16)
            preA = sba("A_pre", [128, 20, 260], BF16)
            cTA = [sba("A_cT%d" % i, [128, 20, 256], BF16) for i in range(2)]
            cTAU = [[U("A_cT%d_%d" % (i, c)) for c in range(20)] for i in range(2)]
            xtA = [sba("A_xt%d" % i, [128, 2, 2048], BF16) for i in range(2)]
            xtAU = [[[U("A_xt%d_%d_%d" % (i, ti, g)) for g in range(4)] for ti in range(2)] for i in range(2)]
            btA = [sba("A_bt%d" % i, [128, 2, 512], BF16) for i in range(2)]
            btAU = [[U("A_bt%d_%d" % (i, ti)) for ti in range(2)] for i in range(2)]
            xdtwA = [sba("A_xdtw%d" % i, [128, 512], BF16) for i in range(2)]
            npair = [0]

            def pair_front(src_d, pos0, G, Sm, zero_left, zero_right, save_tiles):
                par = npair[0] % 2
                npair[0] += 1
                xw_ = xwA[par]
                P.dma(xw_, src_d[:, :, pos0 * 128:pos0 * 128 + 260])
                emit_normmod(cx, nsA, lambda c: xw_[:, c, :], 260, G, Sm, lambda c: hTwA[:, c, :])
                if zero_left:
                    P.memset(hTwA[:, :, 0:2], 0.0)
                if zero_right:
                    P.memset(hTwA[:, :, 258:260], 0.0)
                for ti in range(2):
                    sm = small[ti]
                    ps = cx.bank()
                    for k in range(8):
                        P.mm(ps[:, 0:64], hTwA[:, k, 2 + ti * 128:2 + (ti + 1) * 128], win[:, k, W1_DT:W1_DT + 64],
                             start=(k == 0), stop=(k == 7))
                    P.tt(sm["dtv"], ps[:, 0:64], dtb, ALU.add)
                    P.act(sm["dtv"], sm["dtv"], AF.Exp)
                    P.act(sm["dtv"], sm["dtv"], AF.Ln, bias=1.0)
                    P.tt(sm["dtA"], sm["dtv"], Aneg, ALU.mult)
                    ps2 = cx.bank()
                    dtA = sm["dtA"]
                    P.mm(ps2[:, 64:96], cx.c("GT"), dtA[:, 0:32])
                    P.mm(ps2[:, 128:192], cx.c("ones"), dtA)
                    P.act(sm["w"][:, 0:32], ps2[:, 64:96], AF.Exp)
                    P.act(sm["dec"], ps2[:, 128:192], AF.Exp)
                    P.tt(sm["dtw"][:, 0:32], sm["dtv"][:, 0:32], sm["w"][:, 0:32], ALU.mult)
                for cc in range(20):
                    psp = cx.bank()
                    for k in range(8):
                        P.mm(psp[:, 0:260], win[:, k, cc * 128:(cc + 1) * 128], hTwA[:, k, :], start=(k == 0), stop=(k == 7))
                    P.copy(T(preA.ap[:, cc, :], [preU[cc]]), psp[:, 0:260], eng="act")
                cT_ = cTA[par]
                ctf = lambda cc: T(cT_.ap[:, cc, :], [cTAU[par][cc]])
                for cc in range(20):
                    psc = cx.bank()
                    for j in range(5):
                        P.mm(psc[:, 0:256], diagW[:, cc * 5 + j, :], T(preA.ap[:, cc, j:j + 256], [preU[cc]]),
                             start=(j == 0), stop=(j == 4))
                    P.act(ctf(cc), psc[:, 0:256], AF.Silu, bias=convb[:, cc:cc + 1], scale=1.0)
                out = []
                for ti in range(2):
                    cts = lambda cc, ti=ti: T(cT_.ap[:, cc, ti * 128:(ti + 1) * 128], [cTAU[par][cc]])
                    xtt = lambda g, ti=ti: T(xtA[par].ap[:, ti, g * 512:(g + 1) * 512], [xtAU[par][ti][g]])
                    btt_all = T(btA[par].ap[:, ti, :], [btAU[par][ti]])
                    for g in range(4):
                        pst = cx.bank()
                        for i in range(4):
                            P.mm(pst[:, i * 128:(i + 1) * 128], cts(g * 4 + i), identb)
                        P.copy(xtt(g), pst, eng=("dve" if g % 2 else "act"))
                    pst = cx.bank()
                    for g in range(4):
                        P.mm(pst[:, g * 128:(g + 1) * 128], cts(16 + g), identb)
                    P.copy(btt_all, pst, eng="act")
                    if save_tiles is not None:
                        tt_ = save_tiles[ti]
                        P.dma(xcache[tt_], T(xtA[par].ap[:, ti, :], xtAU[par][ti]), q="pool")
                        P.dma(bcache[tt_, :, 0:512], btt_all, q="pool")
                        P.dma(bcache[tt_, :, 512:1024].rearrange("p (g t) -> p g t", g=4),
                              T(cT_.ap[:, 16:20, ti * 128:(ti + 1) * 128], cTAU[par][16:20]), q="pool")
                    btt = lambda g, ti=ti: T(btA[par].ap[:, ti, g * 128:(g + 1) * 128], [btAU[par][ti]])
                    out.append((small[ti], xtt, btt))
                return out

            def state_pair(src_d, pos0, G, Sm, zl, zr, store_tiles):
                tiles = pair_front(src_d, pos0, G, Sm, zl, zr, store_tiles)
                for ti, (sm, xtt, btt) in enumerate(tiles):
                    for g in range(4):
                        state_step(g, 0, sm, xtt, btt, None if store_tiles is None else store_tiles[ti], xdtw_ring=xdtwA)

            state_pair(d["xc1"], 0, mp["G1c"], mp["S1c"], True, True, None)
            for pos0 in range(0, 16, 2):
                state_pair(d["seq"], pos0, mp["G1x"], mp["S1x"], pos0 == 0, False, (pos0, pos0 + 1))
            P.flush()
        Sbin = sb("s_Sbin", [128, 4, 512], BF16)
        ns = NormScratch(P, s2, 132)
        xw = [sb("s_xw%d" % i, [128, 8, 132]) for i in range(2)]
        hTw = sb("s_hTw", [128, 8, 132], BF16)
        pre = sb("s_pre", [128, 24, 132], BF16)
        cT = [sb("s_cT%d" % i, [128, 24, 128], BF16) for i in range(2)]
        cTU = [[U("s_cT%d_%d" % (i, c)) for c in range(24)] for i in range(2)]
        x_tok = [sb("s_xtok%d" % i, [128, 2048], BF16) for i in range(2)]
        xtU = [[U("s_xt%d_%d" % (i, g)) for g in range(4)] for i in range(2)]
        B_tok = [sb("s_Btok%d" % i, [128, 512], BF16) for i in range(2)]
        segrhs = [sb("s_segrhs%d" % i, [128, 2, 4, 128], BF16) for i in range(4)]
        Dexp = [sb("s_Dexp%d" % i, [128, 4, 128]) for i in range(4)]
        MT = [sb("s_MT%d" % i, [128, 16, 128], BF16) for i in range(2)]
        Gm = [sb("s_Gm%d" % i, [128, 2, 128]) for i in range(2)]
        xdt = [sb("s_xdt%d" % i, [128, 2, 512], BF16) for i in range(2)]
        xdtw = [sb("s_xdtw%d" % i, [128, 512], BF16) for i in range(2)]
        t1 = [sb("s_t1%d" % i, [128, 512]) for i in range(2)]
        yg = [sb("s_yg%d" % i, [128, 512]) for i in range(2)]
        sxm = T(sx_mine, U("d_sxm"))
        sxg = T(sx_gath, U("d_sxg"))
        P.dma(sxm, T(S.ap.rearrange("p g q -> p (g q)"), SU), q="pool")
        P.collective("AllGather", [[0, 1], [2, 3], [4, 5], [6, 7]], sxg, sxm)
        P.flush()
        r0 = sb("s_r0", [128, 2048])
        r1 = sb("s_r1", [128, 2048])
        pmk = sb("s_pmk", [128, 2])
        P.dma(pmk, pmask_d)
        P.dma(r0, sx_gath[0:128, :])
        P.dma(r1, sx_gath[128:256, :])
        Sall = T(S.ap.rearrange("p g q -> p (g q)"), SU)
        P.ts(r0, r0, pmk[:, 0:1], ALU.mult)
        P.stt(Sall, r1, pmk[:, 1:2], r0, ALU.mult, ALU.add)
        for t in range(15, -1, -1):
            full_tile(t)
        P.flush()


def emit_l1_gate_out(cx, st, d, mp, yscr, xmid):
    P = cx.P
    with contextlib.ExitStack() as s2:
        sb = lambda name, shape, dt=F32: P.sb(s2, name, shape, dt)
        wz = sb("g_wz", [128, 8, 2048], BF16)
        wout = sb("g_wout", [128, 16, 1024], BF16)
        ng1 = sb("g_ng1", [128, 16])
        for k in range(8):
            P.dma(wz[:, k, :], d["w_in"][:, k, W1_Z:W1_END], q="pool")
        for k in range(0, 16, 2):
            P.dma(wout[:, k:k + 2, :], d["w_out"][:, k:k + 2, :], q="pool")
        P.dma(ng1, d["ng1"])
        for k in range(16):
            P.ts(wout[:, k, :], wout[:, k, :], ng1[:, k:k + 1], ALU.mult)
        nss = [NormScratch(P, s2, 128) for _ in range(2)]
        bufs = [dict(xw=sb("g_xw%d" % i, [128, 8, 128]), hT=sb("g_hT%d" % i, [128, 8, 128], BF16), yt=sb("g_y%d" % i, [128, 2048]),
                     zs=[sb("g_zs%d_%d" % (i, g), [128, 512]) for g in range(4)],
                     sq=[sb("g_sq%d_%d" % (i, g), [128, 512], BF16) for g in range(4)], yn=sb("g_yn%d" % i, [128, 2048], BF16),
                     ynT=sb("g_ynT%d" % i, [128, 16, 128], BF16), st4=sb("g_st%d" % i, [128, 8]), xo=sb("g_xo%d" % i, [128, 8, 128]))
                for i in range(2)]
        identb = cx.c("ident", bf=True)
        for t in range(DBG.get("l1_tiles", 16)):
            bb = bufs[t % 2]
            ns = nss[t % 2]
            xw, hT, yt, zs, sq, yn, ynT, st4, xo = (bb[k] for k in ("xw", "hT", "yt", "zs", "sq", "yn", "ynT", "st4", "xo"))
            P.dma(xw, d["seq"][:, :, 2 + t * 128:2 + (t + 1) * 128])
            P.dma(yt, yscr[t])
            emit_normmod(cx, ns, lambda c: xw[:, c, :], 128, mp["G1x"], mp["S1x"], lambda c: hT[:, c, :])
            P.memset(st4, 0.0)
            for g in range(4):
                ps = cx.bank()
                for k in range(8):
                    P.mm(ps, hT[:, k, :], wz[:, k, g * 512:(g + 1) * 512], start=(k == 0), stop=(k == 7))
                P.act(zs[g], ps, AF.Silu)
                P.tt(yt[:, g * 512:(g + 1) * 512], yt[:, g * 512:(g + 1) * 512], zs[g], ALU.mult)
                P.act(sq[g], yt[:, g * 512:(g + 1) * 512], AF.Square, accum=st4[:, g:g + 1])
            P.act(st4[:, 4:8], st4[:, 0:4], AF.Sqrt, bias=EPS, scale=1.0 / 512)
            P.recip(st4[:, 4:8], st4[:, 4:8])
            P.tt(yn.v(yn.ap.rearrange("p (g q) -> p g q", g=4)), yt.v(yt.ap.rearrange("p (g q) -> p g q", g=4)),
                 st4.v(st4.ap[:, 4:8].unsqueeze(2).to_broadcast([128, 4, 512])), ALU.mult)
            for q in range(4):
                pst = cx.bank()
                for i in range(4):
                    k = q * 4 + i
                    P.mm(pst[:, i * 128:(i + 1) * 128], yn[:, k * 128:(k + 1) * 128], identb)
                P.copy(ynT.v(ynT.ap[:, q * 4:q * 4 + 4, :]), pst.v(pst.ap.rearrange("p (i t) -> p i t", i=4)), eng="act")
            for half in range(2):
                psy = cx.bank()
                for cc in range(4):
                    c = half * 4 + cc
                    for k in range(16):
                        P.mm(psy[:, cc * 128:(cc + 1) * 128], wout[:, k, c * 128:(c + 1) * 128], ynT[:, k, :],
                             start=(k == 0), stop=(k == 15))
                for cc in range(4):
                    c = half * 4 + cc
                    P.stt(xo[:, c, :], psy[:, cc * 128:(cc + 1) * 128], mp["g1x"][:, c:c + 1], xw[:, c, :], ALU.mult, ALU.add)
            P.dma(xmid[:, :, t * 128:(t + 1) * 128], xo, q="pool")
        P.flush()


def emit_l1_ffn_final(cx, st, d, mp, xmid, y_out):
    P = cx.P
    fng = P.sb(st, "fng", [128, 8], F32)
    zer = P.sb(st, "fzero", [128, 8], F32)
    P.dma(fng, d["fng"])
    P.memset(zer, 0.0)
    for sgi in range(2):
        with contextlib.ExitStack() as s2:
            xr = P.sb(s2, "c_x", [128, 8, 1024], F32)
            xb = Buf(xr.ap, "c_x%d" % sgi, 1024, 512)
            for g in range(2):
                P.dma(xb.tok(g * 512, (g + 1) * 512), xmid[:, :, sgi * 1024 + g * 512:sgi * 1024 + (g + 1) * 512])
            segs = [(lambda c, a=g * 512: xb.tok(a, a + 512, sel=(slice(None), c)), 512, mp["G2x"], mp["S2x"], mp["g2x"])
                    for g in range(2)]
            emit_ffn(cx, s2, d["f_in"], d["f_out"], segs, None)
            ns = NormScratch(P, s2, 512)
            o = P.sb(s2, "c_o", [128, 8, 512], F32)
            for g in range(2):
                emit_normmod(cx, ns, lambda c, a=g * 512: xb.tok(a, a + 512, sel=(slice(None), c)), 512, fng, zer,
                             lambda c: o[:, c, :])
                P.dma(y_out[:, :, sgi * 1024 + g * 512:sgi * 1024 + (g + 1) * 512], o)
            P.flush()


def l1_body(cx, st, d, sscr, yscr, xmid, y_out, stop_after=None):
    P = cx.P
    modv = P.sb(st, "modv1", [128, 48, 2], F32)
    ngs = P.sb(st, "ngs1", [128, 2, 8], F32)
    P.dma(ngs, d["ng"])
    emit_mod(cx, st, d["cvec"], d["modw"], d["modb"], modv)
    mp = emit_modparams(cx, st, modv, ngs, "l1")
    emit_l1_ssd(cx, st, d, mp, sscr, yscr)
    if stop_after == "ssd":
        return
    emit_l1_gate_out(cx, st, d, mp, yscr, xmid)
    if stop_after == "gate":
        return
    emit_l1_ffn_final(cx, st, d, mp, xmid, y_out)


def l1_scratch(nc):
    sscr = nc.dram_tensor("sscr", [16, 128, 4, 512], BF16, kind="Internal").ap()
    dk = "ExternalOutput" if DBG.get("l1dump") else "Internal"
    yscr = nc.dram_tensor("yscr", [16, 128, 2048], F32, kind=dk).ap()
    xmid = nc.dram_tensor("xmid", [128, 8, NTOK], F32, kind=dk).ap()
    return sscr, yscr, xmid


def build_l1(stop_after=None):
    nc = bass.Bass("TRN2", target_bir_lowering=False)
    d = {k: nc.dram_tensor(k, list(s), F32, kind="ExternalInput").ap() for k, s in L1_SHAPES.items()}
    y_out = nc.dram_tensor("y_out", [128, 8, NTOK], F32, kind="ExternalOutput").ap()
    sscr, yscr, xmid = l1_scratch(nc)
    with contextlib.ExitStack() as st:
        P = Prog(nc, st)
        cx = Ctx(nc, P, st)
        cx.cst_d = d["cst"]
        load_consts(cx)
        l1_body(cx, st, d, sscr, yscr, xmid, y_out, stop_after)
    return nc


def make_pm(half):
    pm = np.zeros((128, 64), np.float32)
    r = np.arange(32)
    own_base = 32 * half
    oth_base = 64 + 32 * (1 - half)
    pm[own_base + r, r] = 1.0
    pm[oth_base + r, 63 - r] = 1.0
    return pm


def emit_exchange(cx, st, xres, cres, mine, gaths, seq, xc1, pm_d):
    P = cx.P
    mineU = [U("d_mine%d" % i) for i in range(16)]
    gathT = [T(g, U("d_gath%d" % i)) for i, g in enumerate(gaths)]
    ident = cx.c("ident")
    mine_v = mine.rearrange("(c q) n -> q c n", q=32)
    with contextlib.ExitStack() as s2:
        tok = [P.sb(s2, "x_tok%d" % i, [128, 1024], F32) for i in range(2)]
        zt = P.sb(s2, "x_zero", [128, 8, 2], F32)
        P.memset(zt, 0.0)
        P.dma(seq[:, :, 0:2], zt)
        P.dma(seq[:, :, NSEQ - 2:NSEQ], zt)
        P.dma(xc1[:, :, 0:2], zt)
        P.dma(xc1[:, :, NCTXP - 2:NCTXP], zt)
        P.dma(xc1[:, :, 2:2 + NCTX], cres.all())
        for t in range(16):
            tk = tok[t % 2]
            for hb in range(2):
                ps = cx.bank()
                for cc in range(4):
                    kc = hb * 4 + cc
                    P.mm(ps[:, cc * 128:(cc + 1) * 128], xres.tok(t * 128, (t + 1) * 128, sel=(slice(None), kc)), ident)
                P.copy(tk[:, hb * 512:(hb + 1) * 512], ps, eng=("act" if hb else "dve"))
            for rr in range(2):
                P.dma(T(mine_v[2 * t + rr], [mineU[t]]), tk[rr * 64:(rr + 1) * 64, :], q=("pool" if rr else "sp"))
        for q in range(4):
            P.collective("AllGather", [[0, 1], [2, 3], [4, 5], [6, 7]], gathT[q],
                         T(mine[q * 512:(q + 1) * 512, :], mineU))
        P.flush()
    with contextlib.ExitStack() as s2:
        pm = P.sb(s2, "x_pm", [128, 64], F32)
        P.dma(pm, pm_d)
        NXS = 6
        xs = [P.sb(s2, "x_st%d" % i, [128, 1024], F32) for i in range(NXS)]
        xsU = [[U("x_st%d_%d" % (i, q)) for q in range(4)] for i in range(NXS)]
        stg = [P.sb(s2, "x_stg%d" % i, [128, 8, 512], F32) for i in range(2)]

        def col_rows(r, c):
            q, w = c // 16, c % 16
            return gathT[q].v(gaths[q][r * 512 + w * 32:r * 512 + (w + 1) * 32, :])

        for cj in range(64):
            xa = xs[cj % NXS].ap
            xu = xsU[cj % NXS]
            for r in range(2):
                P.dma(T(xa[r * 32:(r + 1) * 32, :], [xu[r]]), col_rows(r, cj))
                P.dma(T(xa[64 + r * 32:64 + (r + 1) * 32, :], [xu[2 + r]]), col_rows(r, 63 - cj))
            ps = cx.bank()
            for kc in range(8):
                P.mm(ps[:, kc * 64:(kc + 1) * 64], T(xa[:, kc * 128:(kc + 1) * 128], xu), pm)
            sg = stg[(cj // 8) % 2]
            P.copy(sg[:, :, (cj % 8) * 64:(cj % 8 + 1) * 64], ps.v(ps.ap.rearrange("p (k t) -> p k t", k=8)),
                   eng=("act" if cj % 2 else "dve"))
            if cj % 8 == 7:
                P.dma(seq[:, :, 2 + (cj - 7) * 64:2 + (cj + 1) * 64], sg, q="pool")
        P.flush()


FUSED_L1_KEYS = [k for k in L1_SHAPES if k not in ("seq", "xc1", "cst")]


def build_fused():
    nc = bass.Bass("TRN2", target_bir_lowering=False)
    d0 = {k: nc.dram_tensor(k, list(s), F32, kind="ExternalInput").ap() for k, s in L0_SHAPES.items()}
    d1 = {k: nc.dram_tensor("l1_" + k, list(L1_SHAPES[k]), F32, kind="ExternalInput").ap() for k in FUSED_L1_KEYS}
    pm_d = nc.dram_tensor("pm", [128, 64], F32, kind="ExternalInput").ap()
    d1["pmask"] = nc.dram_tensor("pmask", [128, 2], F32, kind="ExternalInput").ap()
    d0["pmask"] = d1["pmask"]
    d1["cst"] = d0["cst"]
    d1["seq"] = nc.dram_tensor("seq", [128, 8, NSEQ], F32, kind="Internal").ap()
    d1["xc1"] = nc.dram_tensor("xc1", [128, 8, NCTXP], F32, kind="Internal").ap()
    mine = nc.dram_tensor("ex_mine", [NTOK, 1024], F32, kind="Internal").ap()
    gath = [nc.dram_tensor("ex_gath%d" % q, [1024, 1024], F32, kind="Internal").ap() for q in range(4)]
    y_out = nc.dram_tensor("y_out", [128, 8, NTOK], F32, kind="ExternalOutput").ap()
    sscr, yscr, xmid = l1_scratch(nc)
    with contextlib.ExitStack() as st:
        P = Prog(nc, st)
        cx = Ctx(nc, P, st)
        cx.cst_d = d0["cst"]
        load_consts(cx)
        with contextlib.ExitStack() as s0:
            xres, cres = l0_body(cx, s0, d0)
            emit_exchange(cx, s0, xres, cres, mine, gath, d1["seq"], d1["xc1"], pm_d)
        l1_body(cx, st, d1, sscr, yscr, xmid, y_out)
    return nc


def prep_fused(inp, b, half):
    m = prep_l0(inp, b, half)
    d1 = prep_l1(inp, None, None, b, half)
    for k in FUSED_L1_KEYS:
        m["l1_" + k] = d1[k]
    m["pm"] = make_pm(half)
    pmask = np.zeros((128, 2), np.float32)
    pmask[:, 1 - half] = 1.0
    m["pmask"] = pmask
    return m


def _run(nc, maps):
    return run_bass_kernel_spmd(nc, maps, core_ids=list(range(8))).results


def kernel_unfused(**inputs):
    inp = {k: np.asarray(v) for k, v in inputs.items()}
    res0 = _run(build_l0(), [prep_l0(inp, c // 2, c % 2) for c in range(8)])
    x1 = np.zeros((4, 4096, 1024), np.float32)
    c1 = np.zeros((4, NCTX, 1024), np.float32)
    for c in range(8):
        b, half = c // 2, c % 2
        xo = unfm(res0[c]["x1"])
        if half == 1:
            xo = xo[::-1]
        x1[b, half * NTOK:(half + 1) * NTOK] = xo
        if half == 0:
            c1[b] = unfm(res0[c]["c1"])
    res1 = _run(build_l1(), [prep_l1(inp, x1[c // 2], c1[c // 2], c // 2, c % 2) for c in range(8)])
    out = np.zeros((4, 4096, 1024), np.float32)
    for b in range(4):
        xcm = np.zeros((4096, 1024), np.float32)
        for half in range(2):
            yo = unfm(res1[b * 2 + half]["y_out"])
            if half == 1:
                yo = yo[::-1]
            xcm[half * NTOK:(half + 1) * NTOK] = yo
        out[b] = xcm.reshape(64, 64, 1024).transpose(1, 0, 2).reshape(4096, 1024)
    return out


def assemble(res1):
    out = np.zeros((4, 4096, 1024), np.float32)
    for b in range(4):
        xcm = np.zeros((4096, 1024), np.float32)
        for half in range(2):
            yo = unfm(res1[b * 2 + half]["y_out"])
            if half == 1:
                yo = yo[::-1]
            xcm[half * NTOK:(half + 1) * NTOK] = yo
        out[b] = xcm.reshape(64, 64, 1024).transpose(1, 0, 2).reshape(4096, 1024)
    return out


def kernel(**inputs):
    inp = {k: np.asarray(v) for k, v in inputs.items()}
    res = _run(build_fused(), [prep_fused(inp, c // 2, c % 2) for c in range(8)])
    return assemble(res)
```

```python
import contextlib
import numpy as np
import concourse.bass as bass
import concourse.mybir as mybir
from concourse.bass_utils import run_bass_kernel_spmd

F32 = mybir.dt.float32
BF16 = mybir.dt.bfloat16
AF = mybir.ActivationFunctionType
ALU = mybir.AluOpType
AX = mybir.AxisListType

ENGS = ("pe", "act", "dve", "pool", "sp")
SAME_ENGINE_SYNC = True
N_DMA_SEMS = 12


class U:
    __slots__ = ("name", "w", "rs", "excl", "ser")

    def __init__(self, name, excl=False):
        self.name = name
        self.w = None
        self.rs = []
        self.excl = excl
        self.ser = 0


class T:
    __slots__ = ("ap", "us", "ser")

    def __init__(self, ap, us, ser=None):
        self.ap = ap
        self.us = us if isinstance(us, (list, tuple)) else [us]
        self.ser = ser

    def __getitem__(self, key):
        return T(self.ap[key], self.us, self.ser)

    def v(self, ap):
        return T(ap, self.us, self.ser)


class Op:
    __slots__ = ("eng", "idx", "fn", "deps", "dma", "sig", "waits", "dsem", "dval", "blk", "inc")

    def __init__(self, eng, idx, fn, deps, dma):
        self.eng = eng
        self.idx = idx
        self.fn = fn
        self.deps = deps
        self.dma = dma
        self.sig = None
        self.waits = []
        self.dsem = None
        self.dval = None


class Prog:
    def __init__(self, nc, stack):
        self.nc = nc
        self.stack = stack
        self.esem = {e: stack.enter_context(nc.semaphore("es_" + e)) for e in ENGS if e != "sp"}
        self.ecount = {e: 0 for e in ENGS}
        self.dsems = [stack.enter_context(nc.semaphore("ds%d" % i)) for i in range(N_DMA_SEMS)]
        self.dcount = [0] * N_DMA_SEMS
        self.dlast = [None] * N_DMA_SEMS
        self.dnext = 0
        self.pending = {e: [] for e in ENGS}
        self.all_ops = []
        self.waited = {e: {} for e in ENGS}
        self.uid = 0
        self.nblocks = 0

    def sb(self, stack, name, shape, dtype, nunits=1):
        self.uid += 1
        name = "%s_%d" % (name, self.uid)
        t = stack.enter_context(self.nc.sbuf_tensor(name, list(shape), dtype))
        return T(t[:] if hasattr(t, "__getitem__") else t.ap(), U(name))

    def ps(self, stack, name, shape, dtype):
        t = stack.enter_context(self.nc.psum_tensor(name, list(shape), dtype))
        return T(t[:], U(name, excl=True))

    def add(self, eng, fn, reads=(), writes=(), dma=False):
        for t in list(reads) + list(writes):
            if t.ser is not None and t.us[0].ser != t.ser:
                raise RuntimeError("PSUM bank %s re-allocated while still live" % t.us[0].name)
        deps = []
        for t in reads:
            for u in t.us:
                if u.w is not None:
                    deps.append(u.w)
                if u.excl:
                    deps.extend(u.rs)
        for t in writes:
            for u in t.us:
                if u.w is not None:
                    deps.append(u.w)
                deps.extend(u.rs)
        deps = [d for d in deps if d.blk == self.nblocks]
        op = Op(eng, len(self.all_ops), fn, deps, dma)
        op.blk = self.nblocks
        self.all_ops.append(op)
        self.pending[eng].append(op)
        for t in reads:
            for u in t.us:
                u.rs.append(op)
        for t in writes:
            for u in t.us:
                u.w = op
                u.rs = []
        if dma:
            s = self.dnext
            self.dnext = (self.dnext + 1) % N_DMA_SEMS
            if self.dlast[s] is not None and self.dlast[s].blk == self.nblocks:
                op.deps.append(self.dlast[s])
            inc = getattr(self, "_next_inc", 16)
            self.dcount[s] += inc
            op.dsem = s
            op.dval = self.dcount[s]
            op.inc = inc
            self.dlast[s] = op
        return op

    def collective(self, kind, groups, out, in_):
        self._next_inc = 1
        try:
            op = self.add("pool", lambda eng: eng.collective_compute(kind, ALU.bypass, replica_groups=groups,
                                                                      ins=[in_.ap.opt()], outs=[out.ap.opt()]),
                          [in_], [out], dma=True)
        finally:
            self._next_inc = 16
        return op

    def flush(self, barrier=True):
        nc = self.nc
        needed = set()
        for e in ENGS:
            for op in self.pending[e]:
                for d in op.deps:
                    if d.dma:
                        continue
                    if d.eng == op.eng and (d.eng == "pe" or not SAME_ENGINE_SYNC) and not op.dma:
                        continue
                    needed.add(d.idx)
        if barrier:
            for e in ENGS:
                if e != "sp" and self.pending[e]:
                    for op in reversed(self.pending[e]):
                        if not op.dma:
                            needed.add(op.idx)
                            break
        for e in ENGS:
            for op in self.pending[e]:
                if op.dma:
                    op.sig = (self.dsems[op.dsem], op.dval)
                elif op.idx in needed and op.sig is None:
                    self.ecount[e] += 1
                    op.sig = (self.esem[e], self.ecount[e])
        for e in ENGS:
            wd = self.waited[e]
            for op in self.pending[e]:
                ws = {}
                for d in op.deps:
                    if not d.dma and d.eng == op.eng and (d.eng == "pe" or not SAME_ENGINE_SYNC) and not op.dma:
                        continue
                    if d.sig is None:
                        raise RuntimeError("dep without signal (emitted in an earlier block?)")
                    sem, val = d.sig
                    k = id(sem)
                    if wd.get(k, 0) >= val:
                        continue
                    if k not in ws or ws[k][1] < val:
                        ws[k] = (sem, val)
                for k, (sem, val) in ws.items():
                    wd[k] = val
                op.waits = list(ws.values())
        finals = []
        if barrier:
            for e in ENGS:
                if e != "sp" and self.ecount[e] > 0:
                    finals.append((self.esem[e], self.ecount[e]))
            for s in range(N_DMA_SEMS):
                if self.dcount[s] > 0:
                    finals.append((self.dsems[s], self.dcount[s]))
        pend = self.pending
        self.pending = {e: [] for e in ENGS}
        waited = self.waited

        def run(eng_name, eng):
            for op in pend[eng_name]:
                for sem, val in op.waits:
                    eng.wait_ge(sem, val)
                ins = op.fn(eng)
                if op.sig is not None:
                    if op.dma:
                        ins.then_inc(op.sig[0], op.inc)
                    else:
                        ins.then_inc(op.sig[0], 1)
            if barrier:
                wd = waited[eng_name]
                for sem, val in finals:
                    if wd.get(id(sem), 0) < val:
                        eng.wait_ge(sem, val)
                        wd[id(sem)] = val

        with nc.Block() as block:
            @block.sync
            def _(eng):
                run("sp", eng)

            @block.tensor
            def _(eng):
                run("pe", eng)

            @block.scalar
            def _(eng):
                run("act", eng)

            @block.vector
            def _(eng):
                run("dve", eng)

            @block.gpsimd
            def _(eng):
                run("pool", eng)
        self.nblocks += 1

    def dma(self, out, in_, q="sp"):
        reads = [in_] if isinstance(in_, T) else []
        writes = [out] if isinstance(out, T) else []
        o = out.ap if isinstance(out, T) else out
        i = in_.ap if isinstance(in_, T) else in_
        return self.add(q, lambda eng: eng.dma_start(out=o, in_=i), reads, writes, dma=True)

    def mm(self, out, lhsT, rhs, start=True, stop=True, extra_reads=()):
        return self.add("pe", lambda eng: eng.matmul(out.ap, lhsT.ap, rhs.ap, start=start, stop=stop),
                        [lhsT, rhs] + list(extra_reads), [out])

    def transpose(self, out, in_, ident):
        return self.add("pe", lambda eng: eng.transpose(out.ap, in_.ap, ident.ap), [in_, ident], [out])

    def act(self, out, in_, func, bias=None, scale=1.0, accum=None, eng="act"):
        reads = [in_]
        kw = {}
        if isinstance(bias, T):
            reads.append(bias)
            kw["bias"] = bias.ap
        elif bias is not None:
            kw["bias"] = bias
        if isinstance(scale, T):
            reads.append(scale)
            kw["scale"] = scale.ap
        else:
            kw["scale"] = scale
        writes = [out]
        if accum is not None:
            kw["accum_out"] = accum.ap
            writes.append(accum)
        return self.add(eng, lambda e: e.activation(out.ap, in_.ap, func, **kw), reads, writes)

    def tt(self, out, a, b, op, eng="dve"):
        return self.add(eng, lambda e: e.tensor_tensor(out.ap, a.ap, b.ap, op), [a, b], [out])

    def ts(self, out, a, s1, op0, s2=None, op1=None, eng="dve", accum=None):
        reads = [a]
        v1 = s1
        if isinstance(s1, T):
            reads.append(s1)
            v1 = s1.ap
        v2 = s2
        if isinstance(s2, T):
            reads.append(s2)
            v2 = s2.ap
        writes = [out]
        kw = {}
        if op1 is not None:
            kw["op1"] = op1
        if accum is not None:
            kw["accum_out"] = accum.ap
            writes.append(accum)
        return self.add(eng, lambda e: e.tensor_scalar(out.ap, a.ap, v1, v2, op0, **kw), reads, writes)

    def stt(self, out, a, s, b, op0, op1, eng="dve"):
        reads = [a, b]
        v = s
        if isinstance(s, T):
            reads.append(s)
            v = s.ap
        return self.add(eng, lambda e: e.scalar_tensor_tensor(out.ap, a.ap, v, b.ap, op0, op1), reads, [out])

    def copy(self, out, in_, eng="dve"):
        if eng == "act":
            return self.add("act", lambda e: e.copy(out.ap, in_.ap), [in_], [out])
        return self.add(eng, lambda e: e.tensor_copy(out.ap, in_.ap), [in_], [out])

    def memset(self, out, val, eng="dve"):
        return self.add(eng, lambda e: e.memset(out.ap, val), [], [out])

    def recip(self, out, in_):
        return self.add("dve", lambda e: e.reciprocal(out.ap, in_.ap), [in_], [out])


DBG = {}
D = 1024
KC = 8
DFF = 2816
NJ = DFF // 128
EPS = 1e-6
NTOK = 2048
NCTX = 256


def make_consts():
    i = np.arange(128)
    s = i[:, None]
    l = i[None, :]
    same = (s // 64) == (l // 64)
    f = lambda a: np.asarray(a, np.float32)
    blocks = [
        ("ident", f(np.eye(128))),
        ("ones", f(np.ones((128, 128)))),
        ("LE", f(s <= l)), ("GE", f(s >= l)), ("GT", f(s > l)), ("LT", f(s < l)),
        ("LE64", f((s <= l) & same)), ("GE64", f((s >= l) & same)),
        ("nGT64", f((s > l) & same) * (-1.0 / 16)), ("nLT64", f((s < l) & same) * (-1.0 / 16)),
        ("nLE64", f((s <= l) & same) * (-1.0 / 16)), ("nGE64", f((s >= l) & same) * (-1.0 / 16)),
        ("nCH", f((s // 64) == np.arange(2)[None, :]) * (-1.0 / 16)),
    ]
    off = {}
    o = 0
    for n, a in blocks:
        off[n] = (o, a.shape[1])
        o += a.shape[1]
    return np.concatenate([a for _, a in blocks], axis=1), off


CST, CST_OFF = make_consts()
NCST = CST.shape[1]


class Buf:
    def __init__(self, ap, name, ntok, gran):
        self.ap = ap
        self.gran = gran
        self.units = [U("%s_%d" % (name, i)) for i in range((ntok + gran - 1) // gran)]

    def tok(self, a, b, sel=None):
        us = self.units[a // self.gran:(b - 1) // self.gran + 1]
        ap = self.ap
        if sel is None:
            return T(ap[..., a:b] if False else _lastslice(ap, a, b), us)
        return T(_lastslice(ap[sel], a, b), us)

    def all(self):
        return T(self.ap, self.units)


def _lastslice(ap, a, b):
    nd = len(ap.shape)
    key = tuple([slice(None)] * (nd - 1) + [slice(a, b)])
    return ap[key]


class Ctx:
    def __init__(self, nc, P, st):
        self.nc = nc
        self.P = P
        self.banks = [P.ps(st, "bank%d" % i, [128, 512], F32) for i in range(8)]
        self.bi = 0
        self.cst = P.sb(st, "cst_sb", [128, NCST], F32)
        self.cstb = P.sb(st, "cst_sbb", [128, NCST], BF16)

    def bank(self):
        b = self.banks[self.bi]
        self.bi = (self.bi + 1) % 8
        b.us[0].ser += 1
        return T(b.ap, b.us, b.us[0].ser)

    def c(self, name, bf=False, rows=128):
        o, w = CST_OFF[name]
        t = self.cstb if bf else self.cst
        return t[0:rows, o:o + w]


def sb_alloc(P, st, name, shape, dtype):
    return P.sb(st, name, shape, dtype)


def emit_mod(cx, st, cvec_d, modw_d, modb_d, modv):
    P = cx.P
    with contextlib.ExitStack() as s2:
        cv = P.sb(s2, "cv", [128, 8, 2], F32)
        scv = P.sb(s2, "scv", [128, 8, 2], BF16)
        mb = P.sb(s2, "mb", [128, 48], F32)
        wb = [P.sb(s2, "mw%d" % i, [128, 8, 512], BF16) for i in range(3)]
        P.dma(cv, cvec_d)
        P.dma(mb, modb_d)
        P.act(scv, cv, AF.Silu)
        ps = cx.bank()
        for g in range(12):
            w = wb[g % 3]
            P.dma(w, modw_d[:, :, g * 512:(g + 1) * 512], q="pool")
            for jj in range(4):
                jc = g * 4 + jj
                for k in range(8):
                    P.mm(ps[:, jc * 2:jc * 2 + 2], w[:, k, jj * 128:(jj + 1) * 128], scv[:, k, :],
                         start=(k == 0), stop=(k == 7))
        psv = ps.v(ps.ap[:, 0:96].rearrange("p (j t) -> p j t", t=2))
        mbv = mb.v(mb.ap.unsqueeze(2).to_broadcast([128, 48, 2]))
        P.tt(modv, psv, mbv, ALU.add)
        P.flush()


def emit_modparams(cx, st, modv, ng_sb, name):
    P = cx.P
    out = {}
    for col, tag in ((0, "x"), (1, "c")):
        for nm, (jscale, jshift, jgate, gi) in (("1", (1, 0, 2, 0)), ("2", (4, 3, 5, 1))):
            G = P.sb(st, "%sG%s%s" % (name, nm, tag), [128, 8], F32)
            S = P.sb(st, "%sS%s%s" % (name, nm, tag), [128, 8], F32)
            g = P.sb(st, "%sg%s%s" % (name, nm, tag), [128, 8], F32)
            sc = modv.v(modv.ap[:, jscale * 8:(jscale + 1) * 8, col])
            sh = modv.v(modv.ap[:, jshift * 8:(jshift + 1) * 8, col])
            ga = modv.v(modv.ap[:, jgate * 8:(jgate + 1) * 8, col])
            P.stt(G, sc, 1.0, ng_sb.v(ng_sb.ap[:, gi, :]), ALU.add, ALU.mult)
            P.copy(S, sh)
            P.copy(g, ga)
            out["G" + nm + tag] = G
            out["S" + nm + tag] = S
            out["g" + nm + tag] = g
    return out


class NormScratch:
    def __init__(self, P, st, N):
        self.N = N
        NormScratch.cnt = getattr(NormScratch, "cnt", 0) + 1
        tg = "n%d_" % NormScratch.cnt
        self.sq = [P.sb(st, tg + "sq%d" % i, [128, N], BF16) for i in range(2)]
        self.rstd = P.sb(st, tg + "rstd", [128, N], F32)
        self.tmp = [P.sb(st, tg + "tmp%d" % i, [128, N], F32) for i in range(2)]


def emit_normmod(cx, ns, src, n, G, S, dst):
    P = cx.P
    ps = cx.bank()
    for c in range(8):
        sq = ns.sq[c % 2][:, 0:n]
        P.act(sq, src(c), AF.Square)
        P.mm(ps[:, 0:n], cx.c("ones", bf=True), sq, start=(c == 0), stop=(c == 7))
    rs = ns.rstd[:, 0:n]
    P.act(rs, ps[:, 0:n], AF.Sqrt, bias=EPS, scale=1.0 / D)
    P.recip(rs, rs)
    for c in range(8):
        tmp = ns.tmp[c % 2][:, 0:n]
        P.stt(tmp, src(c), G[:, c:c + 1], rs, ALU.mult, ALU.mult)
        P.act(dst(c), tmp, AF.Identity, bias=S[:, c:c + 1], scale=1.0)


def emit_ffn(cx, st, f_in_d, f_out_d, segs, ns):
    P = cx.P
    NT = sum(s[1] for s in segs)
    with contextlib.ExitStack() as s2:
        ns = NormScratch(P, s2, 512)
        hT = P.sb(s2, "f_hT", [128, 8, NT], BF16)
        hU = [U("f_hT%d" % i) for i in range(len(segs))]
        act = P.sb(s2, "f_act", [128, NJ, NT], BF16)
        aU = [U("f_act%d" % i) for i in range(len(segs))]
        NWIN = 5
        win = [P.sb(s2, "f_win%d" % i, [128, 8, 256], BF16) for i in range(NWIN)]
        wout = [P.sb(s2, "f_wout%d" % i, [128, NJ, 128], BF16) for i in range(3)]
        sg = [P.sb(s2, "f_sg%d" % i, [128, 512], F32) for i in range(2)]
        offs = []
        o = 0
        for si, (res, n, G, S, gate) in enumerate(segs):
            offs.append(o)
            emit_normmod(cx, ns, res, n, G, S,
                         lambda c, o=o, n=n, si=si: T(hT.ap[:, c, o:o + n], [hU[si]]))
            o += n
        for j in range(NJ):
            w = win[j % NWIN]
            P.dma(w, f_in_d[j], q="pool")
            for si, (res, n, G, S, gate) in enumerate(segs):
                o = offs[si]
                pg = cx.bank()
                pu = cx.bank()
                for k in range(8):
                    P.mm(pg[:, 0:n], w[:, k, 0:128], T(hT.ap[:, k, o:o + n], [hU[si]]), start=(k == 0), stop=(k == 7))
                for k in range(8):
                    P.mm(pu[:, 0:n], w[:, k, 128:256], T(hT.ap[:, k, o:o + n], [hU[si]]), start=(k == 0), stop=(k == 7))
                s_ = sg[(j * len(segs) + si) % 2][:, 0:n]
                P.act(s_, pg[:, 0:n], AF.Silu)
                P.tt(T(act.ap[:, j, o:o + n], [aU[si]]), s_, pu[:, 0:n], ALU.mult)
        for c in range(8):
            w = wout[c % 3]
            P.dma(w, f_out_d[c], q="pool")
            for si, (res, n, G, S, gate) in enumerate(segs):
                o = offs[si]
                py = cx.bank()
                for j in range(NJ):
                    P.mm(py[:, 0:n], w[:, j, :], T(act.ap[:, j, o:o + n], [aU[si]]), start=(j == 0), stop=(j == NJ - 1))
                r = res(c)
                P.stt(r, py[:, 0:n], gate[:, c:c + 1], r, ALU.mult, ALU.add)
        P.flush()


W0_Q, W0_K, W0_R, W0_U, W0_LR, W0_KT, W0_VT, W0_GT, W0_END = 0, 256, 512, 1024, 1536, 1568, 1824, 2336, 2848


def emit_l0_mixer(cx, st, d, xres, cres, mp, ns):
    P = cx.P
    with contextlib.ExitStack() as s2:
        sb = lambda name, shape, dt=F32: P.sb(s2, name, shape, dt)
        win = sb("a_win", [128, 8, W0_END], BF16)
        wout = sb("a_wout", [128, 8, 1024], BF16)
        gate = sb("a_gate", [33, 512], BF16)
        wsp = sb("a_ws", [128, 4, 128], BF16)
        sbi = sb("a_sb", [1, 512], BF16)
        gng = sb("a_gng", [128, 4])
        vng = sb("a_vng", [128, 512])
        for k in range(8):
            P.dma(win[:, k, :], d["w_in"][:, k, :], q="pool")
        for k in range(0, 8, 2):
            P.dma(wout[:, k:k + 2, :], d["w_out"][:, k:k + 2, :], q="pool")
        P.dma(gate, d["gate"], q="pool")
        P.dma(wsp, d["ws"], q="pool")
        P.dma(sbi, d["sb"], q="pool")
        P.dma(gng, d["gng"])
        P.dma(vng, d["vng"])
        Sf = [sb("a_Sf%d" % p, [128, 128]) for p in range(2)]
        Sb = [sb("a_Sb%d" % p, [128, 128]) for p in range(2)]
        NTL = 18
        Sbin = P.sb(s2, "a_Sbin", [128, NTL * 4, 128], BF16)
        SbinU = [U("Sbin%d" % i) for i in range(NTL)]
        sbin = lambda t, j, p: T(Sbin.ap[:, t * 4 + j * 2 + p, :], [SbinU[t]])
        Sfbf = [[sb("a_Sfbf%d%d" % (j, p), [128, 128], BF16) for p in range(2)] for j in range(2)]
        ns = NormScratch(P, s2, 128)
        xin = sb("a_xin", [128, 8, 128])
        hT = sb("a_hT", [128, 8, 128], BF16)
        lr1 = sb("a_lr1", [33, 128], BF16)
        v_tok = sb("a_vtok", [128, 512], BF16)
        e_t = sb("a_e", [128, 512])
        la = sb("a_la", [128, 512])
        expc = sb("a_expc", [128, 512])
        kwf = sb("a_kwf", [128, 256], BF16)
        kwb = sb("a_kwb", [128, 256], BF16)
        Ep = sb("a_Ep", [128, 512])
        Em = sb("a_Em", [128, 512])
        qd = sb("a_qd", [128, 4, 128], BF16)
        kd = sb("a_kd", [128, 4, 128], BF16)
        dec = sb("a_dec", [128, 4, 2])
        Pf = sb("a_Pf", [128, 4, 128], BF16)
        Pb = sb("a_Pb", [128, 4, 128], BF16)
        osq = sb("a_osq", [128, 512])
        orstd = sb("a_orstd", [128, 512])
        a1 = e_t
        rs = expc
        gel = Em
        vnb = sb("a_vnb", [128, 512], BF16)
        abT = sb("a_abT", [128, 8, 128], BF16)
        st8 = sb("a_st8", [128, 8])
        P.memset(lr1, 1.0)
        for p in range(2):
            P.memset(Sf[p], 0.0)
            P.memset(Sb[p], 0.0)

        def front(src, G, S):
            emit_normmod(cx, ns, src, 128, G, S, lambda c: hT[:, c, :])

        def gates(lo, hi):
            ps_lr = cx.bank()
            for k in range(8):
                P.mm(ps_lr[0:32, 0:128], win[:, k, W0_LR:W0_LR + 32], hT[:, k, :], start=(k == 0), stop=(k == 7))
            P.copy(lr1[0:32, :], ps_lr[0:32, 0:128], eng="act")
            ps_z = cx.bank()
            P.mm(ps_z, lr1, gate)
            P.act(e_t[:, lo:hi], ps_z[:, lo:hi], AF.Exp, scale=-1.0)
            P.act(la[:, lo:hi], e_t[:, lo:hi], AF.Ln, bias=1.0)

        def proj_tok(c0, n):
            ps = cx.bank()
            for k in range(8):
                P.mm(ps[:, 0:n], hT[:, k, :], win[:, k, c0:c0 + n], start=(k == 0), stop=(k == 7))
            return ps

        def proj_fm(c0, nchunks):
            ps = cx.bank()
            for i in range(nchunks):
                for k in range(8):
                    P.mm(ps[:, i * 128:(i + 1) * 128], win[:, k, c0 + i * 128:c0 + (i + 1) * 128], hT[:, k, :],
                         start=(k == 0), stop=(k == 7))
            return ps

        def state_update(S, decT, slot, j, ps_d):
            for hh in range(2):
                r0 = hh * 64
                P.stt(S[r0:r0 + 64, :], S[r0:r0 + 64, :], decT[r0:r0 + 64, slot, j:j + 1],
                      ps_d[r0:r0 + 64, hh * 128:(hh + 1) * 128], ALU.mult, ALU.add)

        def pass1_tile(src, G, S, store_idx, rd):
            front(src, G, S)
            gates(rd * 256, rd * 256 + 256)
            lad = la[:, rd * 256:(rd + 1) * 256]
            ps_c = cx.bank()
            P.mm(ps_c[:, 0:256], cx.c("nLT64" if rd else "nGT64"), lad)
            P.act(expc[:, 0:256], ps_c[:, 0:256], AF.Exp)
            ps_k = proj_tok(W0_KT, 256)
            P.tt(kwf, ps_k[:, 0:256], expc[:, 0:256], ALU.mult)
            ps_v = proj_tok(W0_VT, 512)
            P.copy(v_tok, ps_v, eng="act")
            ps_t = cx.bank()
            for p in range(2):
                P.mm(ps_t[:, p * 2:p * 2 + 2], la[:, rd * 256 + p * 128:rd * 256 + (p + 1) * 128], cx.c("nCH"))
            decv = dec.v(dec.ap[:, 0:2, :])
            P.act(decv, ps_t.v(ps_t.ap[:, 0:4].rearrange("p (a b) -> p a b", b=2)), AF.Exp)
            for p in range(2):
                for j in ((1, 0) if rd else (0, 1)):
                    ps_d = cx.bank()
                    P.mm(ps_d[:, 0:256], kwf[64 * j:64 * j + 64, p * 128:(p + 1) * 128],
                         v_tok[64 * j:64 * j + 64, p * 256:(p + 1) * 256])
                    if store_idx is not None:
                        P.copy(sbin(store_idx, j, p), Sf[p], eng="act")
                    state_update(Sf[p], dec, p, j, ps_d)

        def pass2_tile(src, G, S, res, gatev, store_idx, rd):
            sd = 1 - rd
            front(src, G, S)
            gates(0, 512)
            ps_c = cx.bank()
            P.mm(ps_c[:, 0:256], cx.c("nGT64"), la[:, 0:256])
            P.mm(ps_c[:, 256:512], cx.c("nLT64"), la[:, 256:512])
            P.act(expc, ps_c, AF.Exp)
            ps_k = proj_tok(W0_KT, 256)
            P.tt(kwf, ps_k[:, 0:256], expc[:, rd * 256:(rd + 1) * 256], ALU.mult)
            ps_v = proj_tok(W0_VT, 512)
            P.copy(v_tok, ps_v, eng="act")
            ps_b = cx.bank()
            for p in range(2):
                for dr in range(2):
                    slot = p * 2 + dr
                    lsl = la[:, dr * 256 + p * 128:dr * 256 + (p + 1) * 128]
                    P.mm(ps_b[:, slot * 128:(slot + 1) * 128], lsl, cx.c("nLE64" if dr == 0 else "nGE64"))
            P.act(Ep, ps_b, AF.Exp)
            P.act(Em, ps_b, AF.Exp, scale=-1.0)
            ps_t = cx.bank()
            for p in range(2):
                for dr in range(2):
                    slot = p * 2 + dr
                    lsl = la[:, dr * 256 + p * 128:dr * 256 + (p + 1) * 128]
                    P.mm(ps_t[:, slot * 2:slot * 2 + 2], lsl, cx.c("nCH"))
            P.act(dec, ps_t.v(ps_t.ap[:, 0:8].rearrange("p (a b) -> p a b", b=2)), AF.Exp)
            ps_q = proj_fm(W0_Q, 2)
            for p in range(2):
                q3 = ps_q.v(ps_q.ap[:, p * 128:(p + 1) * 128].unsqueeze(1).to_broadcast([128, 2, 128]))
                Ep3 = Ep.v(Ep.ap[:, p * 256:(p + 1) * 256].rearrange("p (b t) -> p b t", b=2))
                P.stt(qd[:, 2 * p:2 * p + 2, :], q3, 0.125, Ep3, ALU.mult, ALU.mult)
            ps_kT = proj_fm(W0_K, 2)
            for p in range(2):
                k3 = ps_kT.v(ps_kT.ap[:, p * 128:(p + 1) * 128].unsqueeze(1).to_broadcast([128, 2, 128]))
                Em3 = Em.v(Em.ap[:, p * 256:(p + 1) * 256].rearrange("p (b t) -> p b t", b=2))
                P.tt(kd[:, 2 * p:2 * p + 2, :], k3, Em3, ALU.mult)
            pv = lambda t_, par: t_.v(t_.ap.rearrange("p (i two t) -> p two i t", two=2, t=128)[:, par])
            mF = cx.c("LE64")
            mB = cx.c("GE64")
            for dr in range(2):
                ps2 = [cx.bank(), cx.bank()]
                for h in range(4):
                    p, b0, par, i = h // 2, 64 * (h % 2), h % 2, h // 2
                    P.mm(ps2[par][:, i * 128:(i + 1) * 128], kd[b0:b0 + 64, p * 2 + dr, :], qd[b0:b0 + 64, p * 2 + dr, :])
                for par in range(2):
                    dst = (Pf if dr == 0 else Pb)[:, par * 2:par * 2 + 2, :]
                    msk = mF if dr == 0 else mB
                    P.tt(dst, ps2[par].v(ps2[par].ap[:, 0:256].rearrange("p (h t) -> p h t", h=2)),
                         msk.v(msk.ap.unsqueeze(1).to_broadcast([128, 2, 128])), ALU.mult)
            for p in range(2):
                for j in ((1, 0) if rd else (0, 1)):
                    P.copy(Sfbf[j][p], Sf[p], eng="act")
                    ps_d = cx.bank()
                    P.mm(ps_d[:, 0:256], kwf[64 * j:64 * j + 64, p * 128:(p + 1) * 128],
                         v_tok[64 * j:64 * j + 64, p * 256:(p + 1) * 256])
                    state_update(Sf[p], dec, p * 2 + rd, j, ps_d)
            ps_o2 = [cx.bank(), cx.bank()]
            for h in range(4):
                p, b0, par, i = h // 2, 64 * (h % 2), h % 2, h // 2
                c0 = i * 128
                po = ps_o2[par]
                P.mm(po[:, c0:c0 + 128], v_tok[:, h * 128:(h + 1) * 128], Pf[:, par * 2 + i, :], start=True, stop=False)
                P.mm(po[:, c0:c0 + 128], v_tok[:, h * 128:(h + 1) * 128], Pb[:, par * 2 + i, :], start=False, stop=False)
                for j in (0, 1):
                    P.mm(po[:, c0 + 64 * j:c0 + 64 * j + 64], Sfbf[j][p][b0:b0 + 64, :],
                         qd[b0:b0 + 64, p * 2 + rd, 64 * j:64 * j + 64], start=False, stop=False)
                    P.mm(po[:, c0 + 64 * j:c0 + 64 * j + 64], sbin(store_idx, j, p)[b0:b0 + 64, :],
                         qd[b0:b0 + 64, p * 2 + sd, 64 * j:64 * j + 64], start=False, stop=(j == 1))
            for par in range(2):
                P.act(osq[:, par * 256:(par + 1) * 256], ps_o2[par][:, 0:256], AF.Square)
            ps_n = cx.bank()
            P.mm(ps_n, cx.c("ones"), osq)
            P.act(orstd, ps_n, AF.Sqrt, bias=EPS, scale=1.0 / 128)
            P.recip(orstd, orstd)
            for par in range(2):
                P.tt(a1[:, par * 256:(par + 1) * 256], ps_o2[par][:, 0:256], orstd[:, par * 256:(par + 1) * 256], ALU.mult)
            ps_r = proj_fm(W0_R, 4)
            P.act(rs, ps_r, AF.Silu)
            for par in range(2):
                a1p = a1[:, par * 256:(par + 1) * 256]
                a1p3 = a1p.v(a1p.ap.rearrange("p (i t) -> p i t", i=2))
                gp = gng.v(gng.ap.rearrange("p (i two) -> p two i", two=2)[:, par].unsqueeze(2).to_broadcast([128, 2, 128]))
                P.tt(a1p3, a1p3, gp, ALU.mult)
                ab_par = abT.v(abT.ap[:, 0:4, :].rearrange("p (i two) t -> p two i t", two=2)[:, par])
                P.tt(ab_par, a1p3, pv(rs, par), ALU.mult)
            ps_g = proj_tok(W0_GT, 512)
            P.memset(st8, 0.0)
            P.act(gel, ps_g, AF.Gelu_apprx_tanh, accum=st8[:, 0:1])
            P.act(osq, gel, AF.Square, accum=st8[:, 1:2])
            P.ts(st8[:, 2:3], st8[:, 0:1], 1.0 / 512, ALU.mult)
            P.tt(st8[:, 3:4], st8[:, 2:3], st8[:, 2:3], ALU.mult)
            P.stt(st8[:, 4:5], st8[:, 1:2], 1.0 / 512, st8[:, 3:4], ALU.mult, ALU.subtract)
            P.act(st8[:, 5:6], st8[:, 4:5], AF.Sqrt, bias=EPS, scale=1.0)
            P.recip(st8[:, 5:6], st8[:, 5:6])
            P.stt(st8[:, 6:7], st8[:, 2:3], -1.0, st8[:, 5:6], ALU.mult, ALU.mult)
            P.act(osq, gel, AF.Identity, bias=st8[:, 6:7], scale=st8[:, 5:6])
            P.tt(vnb, osq, vng, ALU.mult)
            ps_u = proj_fm(W0_U, 4)
            P.act(gel, ps_u, AF.Gelu_apprx_tanh)
            ps_s = cx.bank()
            for g in range(4):
                P.mm(ps_s[:, g * 128:(g + 1) * 128], vnb[:, g * 128:(g + 1) * 128], wsp[:, g, :], start=True, stop=False)
                P.mm(ps_s[:, g * 128:(g + 1) * 128], cx.c("ones", bf=True, rows=1), sbi[:, g * 128:(g + 1) * 128],
                     start=False, stop=True)
            P.tt(abT.v(abT.ap[:, 4:8, :]), gel.v(gel.ap.rearrange("p (h t) -> p h t", h=4)),
                 ps_s.v(ps_s.ap.rearrange("p (h t) -> p h t", h=4)), ALU.mult)
            if DBG.get("dump") and store_idx == DBG.get("dump_tile", 2):
                dd = DBG["dbg_ap"]
                dt_ = sb("a_dbg", [128, 1024])
                P.copy(dt_.v(dt_.ap[:, 0:1024].rearrange("p (c t) -> p c t", c=8)), hT)
                P.dma(dd[:, 0:1024], dt_)
                P.copy(dt_[:, 0:512], la)
                P.copy(dt_[:, 512:1024], a1)
                P.dma(dd[:, 1024:2048], dt_)
                P.copy(dt_.v(dt_.ap[:, 0:1024].rearrange("p (c t) -> p c t", c=8)), abT)
                P.dma(dd[:, 2048:3072], dt_)
            for half in range(2):
                ps_y = cx.bank()
                for cc in range(4):
                    c = half * 4 + cc
                    for k in range(8):
                        P.mm(ps_y[:, cc * 128:(cc + 1) * 128], wout[:, k, c * 128:(c + 1) * 128], abT[:, k, :],
                             start=(k == 0), stop=(k == 7))
                for cc in range(4):
                    c = half * 4 + cc
                    r = res(c)
                    P.stt(r, ps_y[:, cc * 128:(cc + 1) * 128], gatev[:, c:c + 1], r, ALU.mult, ALU.add)

        for t in (0, 1):
            pass1_tile(lambda c, t=t: cres.tok(t * 128, (t + 1) * 128, sel=(slice(None), c)), mp["G1c"], mp["S1c"], t, 0)
        for t in range(16):
            pass1_tile(lambda c, t=t: xres.tok(t * 128, (t + 1) * 128, sel=(slice(None), c)), mp["G1x"], mp["S1x"], 2 + t, 0)
        nc = cx.nc
        gx_mine = nc.dram_tensor("gx_mine", [128, 256], F32, kind="Internal").ap()
        gx_gath = nc.dram_tensor("gx_gath", [256, 256], F32, kind="Internal").ap()
        gxm = T(gx_mine, U("d_gxm"))
        gxg = T(gx_gath, U("d_gxg"))
        for p in range(2):
            P.dma(gxm.v(gx_mine[:, p * 128:(p + 1) * 128]), Sf[p], q="pool")
        P.collective("AllGather", [[0, 1], [2, 3], [4, 5], [6, 7]], gxg, gxm)
        rcv = sb("a_rcv", [128, 2, 256])
        pmk = sb("a_pmk", [128, 2])
        P.dma(pmk, d["pmask"])
        P.dma(rcv, gxg.v(gx_gath.rearrange("(r q) n -> q r n", r=2)))
        P.ts(rcv[:, 0, :], rcv[:, 0, :], pmk[:, 0:1], ALU.mult)
        P.stt(rcv[:, 1, :], rcv[:, 1, :], pmk[:, 1:2], rcv[:, 0, :], ALU.mult, ALU.add)
        for p in range(2):
            P.memset(Sf[p], 0.0)
        for t in (1, 0):
            rv = lambda c, t=t: cres.tok(t * 128, (t + 1) * 128, sel=(slice(None), c))
            pass2_tile(rv, mp["G1c"], mp["S1c"], rv, mp["g1c"], t, 1)
        for p in range(2):
            P.copy(Sf[p], rcv[:, 1, p * 128:(p + 1) * 128])
        for t in range(DBG.get("np2", 16) - 1, -1, -1):
            rv = lambda c, t=t: xres.tok(t * 128, (t + 1) * 128, sel=(slice(None), c))
            pass2_tile(rv, mp["G1x"], mp["S1x"], rv, mp["g1x"], 2 + t, 1)
        P.flush()


def fm(a):
    a = np.asarray(a, np.float32)
    n = a.shape[0]
    return np.ascontiguousarray(a.T.reshape(8, 128, n).transpose(1, 0, 2))


def unfm(a):
    n = a.shape[2]
    return np.ascontiguousarray(a.transpose(1, 0, 2).reshape(1024, n).T)


def wk(w):
    w = np.asarray(w, np.float32)
    kc = w.shape[0] // 128
    return np.ascontiguousarray(w.reshape(kc, 128, w.shape[1]).transpose(1, 0, 2))


def vec_fm(v):
    v = np.asarray(v, np.float32)
    return np.ascontiguousarray(v.reshape(-1, 128).T)


def prep_common(inp, layer, b):
    d = {}
    d["cvec"] = np.ascontiguousarray(np.stack([vec_fm(inp["c"][b]), vec_fm(inp["c_ctx"])], axis=2))
    d["modw"] = wk(inp["mod_w"][layer])
    d["modb"] = vec_fm(inp["mod_b"][layer])
    d["ng"] = np.ascontiguousarray(np.stack([vec_fm(inp["norm_g"][layer, 0]), vec_fm(inp["norm_g"][layer, 1])], axis=1))
    fw_in = np.asarray(inp["ffn_w_in"][layer], np.float32)
    gu = np.stack([fw_in[:, :DFF].reshape(1024, NJ, 128), fw_in[:, DFF:].reshape(1024, NJ, 128)], axis=2)
    gu = gu.reshape(8, 128, NJ, 256).transpose(2, 1, 0, 3)
    d["f_in"] = np.ascontiguousarray(gu)
    fo = np.asarray(inp["ffn_w_out"][layer], np.float32).reshape(NJ, 128, 8, 128).transpose(2, 1, 0, 3)
    d["f_out"] = np.ascontiguousarray(fo)
    d["cst"] = CST
    return d


def prep_l0(inp, b, half):
    d = prep_common(inp, 0, b)
    x = np.asarray(inp["x"][b], np.float32)
    ctx = np.asarray(inp["ctx"][b], np.float32)
    own = x[half * NTOK:(half + 1) * NTOK]
    oth = x[(1 - half) * NTOK:(2 - half) * NTOK]
    if half == 1:
        own, oth, ctx = own[::-1], oth[::-1], ctx[::-1]
    d["xo"], d["xx"], d["xc"] = fm(own), fm(oth), fm(ctx)
    w = np.asarray(inp["ab_w_in"][0], np.float32)
    k, v, lrf, lrb, q, r, u, g = np.split(w, np.cumsum([256, 512, 16, 16, 256, 512, 512])[:-1].tolist() + [2080], axis=1)
    gw = np.asarray(inp["ab_gate_w"][0], np.float32)
    gb = np.asarray(inp["ab_gate_b"][0], np.float32)
    if half == 1:
        lrf, lrb = lrb, lrf
        gw, gb = gw[::-1], gb[::-1]
    wcat = np.concatenate([q, k, r, u, lrf, lrb, k, v, g], axis=1)
    assert wcat.shape[1] == W0_END
    d["w_in"] = wk(wcat)
    gt = np.zeros((33, 512), np.float32)
    gt[0:16, 0:256] = gw[0]
    gt[16:32, 256:512] = gw[1]
    gt[32, 0:256] = gb[0]
    gt[32, 256:512] = gb[1]
    d["gate"] = gt
    d["gng"] = vec_fm(inp["ab_gla_norm_g"][0])
    d["vng"] = np.ascontiguousarray(np.broadcast_to(np.asarray(inp["ab_vnorm_g"][0], np.float32)[None, :], (128, 512)))
    sw = np.asarray(inp["ab_spatial_w"][0], np.float32)
    sbv = np.asarray(inp["ab_spatial_b"][0], np.float32)
    if half == 1:
        sw = sw[:, ::-1, ::-1]
        sbv = sbv[:, ::-1]
    d["ws"] = np.ascontiguousarray(sw.transpose(2, 0, 1))
    d["sb"] = np.ascontiguousarray(sbv.reshape(1, 512))
    d["w_out"] = wk(inp["ab_w_out"][0])
    return d


L0_SHAPES = dict(xo=[128, 8, NTOK], xx=[128, 8, NTOK], xc=[128, 8, NCTX], cvec=[128, 8, 2], modw=[128, 8, 6144],
                 modb=[128, 48], ng=[128, 2, 8], w_in=[128, 8, W0_END], gate=[33, 512], gng=[128, 4], vng=[128, 512],
                 ws=[128, 4, 128], sb=[1, 512], w_out=[128, 8, 1024], f_in=[NJ, 128, 8, 256], f_out=[8, 128, NJ, 128],
                 cst=[128, NCST])


def load_consts(cx):
    P = cx.P
    P.dma(cx.cst, cx.cst_d)
    P.dma(cx.cstb, cx.cst_d, q="pool")


def l0_body(cx, s0, d, stop_after=None):
    P = cx.P
    xr = P.sb(s0, "xres", [128, 8, NTOK], F32)
    xres = Buf(xr.ap, "xres", NTOK, 128)
    cr = P.sb(s0, "cres", [128, 8, NCTX], F32)
    cres = Buf(cr.ap, "cres", NCTX, 128)
    for g in range(4):
        P.dma(xres.tok(g * 512, (g + 1) * 512), d["xo"][:, :, g * 512:(g + 1) * 512])
    P.dma(cres.all(), d["xc"])
    modv = P.sb(s0, "modv", [128, 48, 2], F32)
    ngs = P.sb(s0, "ngs", [128, 2, 8], F32)
    P.dma(ngs, d["ng"])
    emit_mod(cx, s0, d["cvec"], d["modw"], d["modb"], modv)
    mp = emit_modparams(cx, s0, modv, ngs, "l0")
    if stop_after != "mod":
        emit_l0_mixer(cx, s0, d, xres, cres, mp, None)
    if stop_after not in ("mod", "mixer"):
        for sgi in range(2):
            segs = []
            for g in range(2):
                a = sgi * 1024 + g * 512
                segs.append((lambda c, a=a: xres.tok(a, a + 512, sel=(slice(None), c)), 512,
                             mp["G2x"], mp["S2x"], mp["g2x"]))
            a = sgi * 128
            segs.append((lambda c, a=a: cres.tok(a, a + 128, sel=(slice(None), c)), 128,
                         mp["G2c"], mp["S2c"], mp["g2c"]))
            emit_ffn(cx, s0, d["f_in"], d["f_out"], segs, None)
    return xres, cres


def build_l0(stop_after=None):
    nc = bass.Bass("TRN2", target_bir_lowering=False)
    d = {k: nc.dram_tensor(k, list(s), F32, kind="ExternalInput").ap() for k, s in L0_SHAPES.items()}
    x1 = nc.dram_tensor("x1", [128, 8, NTOK], F32, kind="ExternalOutput").ap()
    c1 = nc.dram_tensor("c1", [128, 8, NCTX], F32, kind="ExternalOutput").ap()
    if DBG.get("dump"):
        DBG["dbg_ap"] = nc.dram_tensor("dbg", [128, 3072], F32, kind="ExternalOutput").ap()
    with contextlib.ExitStack() as st:
        P = Prog(nc, st)
        cx = Ctx(nc, P, st)
        cx.cst_d = d["cst"]
        load_consts(cx)
        xres, cres = l0_body(cx, st, d, stop_after)
        for g in range(4):
            P.dma(x1[:, :, g * 512:(g + 1) * 512], xres.tok(g * 512, (g + 1) * 512))
        P.dma(c1, cres.all())
        P.flush()
    return nc


W1_X, W1_B, W1_C, W1_DT, W1_Z, W1_END = 0, 2048, 2560, 3072, 3136, 5184
NSEQ = 2 * NTOK + 4
NCTXP = NCTX + 4

L1_SHAPES = dict(seq=[128, 8, NSEQ], xc1=[128, 8, NCTXP], cvec=[128, 8, 2], modw=[128, 8, 6144], modb=[128, 48],
                 ng=[128, 2, 8], w_in=[128, 8, W1_END], convw=[128, 24, 5], convb=[128, 24], dtb=[128, 64],
                 alog=[128, 64], dsk=[128, 32], ng1=[128, 16], w_out=[128, 16, 1024], fng=[128, 8],
                 f_in=[NJ, 128, 8, 256], f_out=[8, 128, NJ, 128], cst=[128, NCST])


def prep_l1(inp, x1b, ctx1b, b, half):
    d = prep_common(inp, 1, b)
    if x1b is not None:
        xcm = np.asarray(x1b, np.float32).reshape(64, 64, 1024).transpose(1, 0, 2).reshape(4096, 1024)
        own = xcm[half * NTOK:(half + 1) * NTOK]
        oth = xcm[(1 - half) * NTOK:(2 - half) * NTOK]
        ctx = np.asarray(ctx1b, np.float32)
        if half == 1:
            own, oth, ctx = own[::-1], oth[::-1], ctx[::-1]
        z2 = np.zeros((2, 1024), np.float32)
        d["seq"] = fm(np.concatenate([z2, own, oth, z2], axis=0))
        d["xc1"] = fm(np.concatenate([z2, ctx, z2], axis=0))
    w = np.asarray(inp["ssd_w_in"][0], np.float32)
    cw = np.asarray(inp["ssd_conv_w"][0], np.float32)
    dtb = np.asarray(inp["ssd_dt_bias"][0], np.float32)
    alog = np.asarray(inp["ssd_a_log"][0], np.float32)
    if half == 1:
        w = np.concatenate([w[:, :W1_DT], w[:, W1_DT + 32:W1_DT + 64], w[:, W1_DT:W1_DT + 32], w[:, W1_Z:]], axis=1)
        cw = cw[::-1]
        dtb = dtb[::-1]
        alog = alog[::-1]
    d["w_in"] = wk(w)
    d["convw"] = np.ascontiguousarray(cw.T.reshape(24, 128, 5).transpose(1, 0, 2))
    d["convb"] = vec_fm(inp["ssd_conv_b"][0])
    d["dtb"] = np.ascontiguousarray(np.broadcast_to(dtb.reshape(1, 64), (128, 64)))
    d["alog"] = np.ascontiguousarray(np.broadcast_to(alog.reshape(1, 64), (128, 64)))
    d["dsk"] = np.ascontiguousarray(np.broadcast_to(np.asarray(inp["ssd_d"][0], np.float32).reshape(1, 32), (128, 32)))
    d["ng1"] = vec_fm(inp["ssd_norm_g"][0])
    d["w_out"] = wk(inp["ssd_w_out"][0])
    d["fng"] = vec_fm(inp["final_norm_g"])
    return d


def emit_l1_ssd(cx, st, d, mp, sscr, yscr):
    xcache = cx.nc.dram_tensor("xcache", [16, 128, 2048], BF16, kind="Internal").ap()
    bcache = cx.nc.dram_tensor("bcache", [16, 128, 1024], BF16, kind="Internal").ap()
    sx_mine = cx.nc.dram_tensor("sx_mine", [128, 2048], F32, kind="Internal").ap()
    sx_gath = cx.nc.dram_tensor("sx_gath", [256, 2048], F32, kind="Internal").ap()
    return _emit_l1_ssd(cx, st, d, mp, sscr, yscr, xcache, bcache, sx_mine, sx_gath, d["pmask"])


def _emit_l1_ssd(cx, st, d, mp, sscr, yscr, xcache, bcache, sx_mine, sx_gath, pmask_d):
    P = cx.P
    with contextlib.ExitStack() as s2:
        sb = lambda name, shape, dt=F32: P.sb(s2, name, shape, dt)
        NCOL = W1_Z
        win = sb("s_win", [128, 8, NCOL], BF16)
        for k in range(8):
            P.dma(win[:, k, :], d["w_in"][:, k, 0:NCOL], q="pool")
        convw = sb("s_convw", [128, 24, 5])
        convb = sb("s_convb", [128, 24])
        dtb = sb("s_dtb", [128, 64])
        Aneg = sb("s_Aneg", [128, 64])
        dsk = sb("s_dsk", [128, 32])
        P.dma(convw, d["convw"])
        P.dma(convb, d["convb"])
        P.dma(dtb, d["dtb"])
        P.dma(Aneg, d["alog"])
        P.dma(dsk, d["dsk"])
        P.act(Aneg, Aneg, AF.Exp)
        P.ts(Aneg, Aneg, -1.0, ALU.mult)
        diagW = sb("s_diag", [128, 120, 128], BF16)
        identb = cx.c("ident", bf=True)
        for cc in range(24):
            for j in range(5):
                P.ts(diagW[:, cc * 5 + j, :], identb, convw[:, cc, j:j + 1], ALU.mult)
        S = sb("s_S", [128, 4, 512])
        SU = [U("s_S%d" % g) for g in range(4)]
        Sg_ = lambda g: T(S.ap[:, g, :], [SU[g]])
        Sbf = sb("s_Sbf", [128, 4, 512], BF16)
        SbfU = [U("s_Sbf%d" % g) for g in range(4)]
        Sbfg = lambda g: T(Sbf.ap[:, g, :], [SbfU[g]])
        small = [dict(dtv=sb("s_dt%d" % i, [128, 64]), dtA=sb("s_dtA%d" % i, [128, 64]), e=sb("s_e%d" % i, [128, 64]),
                      w=sb("s_w%d" % i, [128, 64]), dec=sb("s_dec%d" % i, [128, 64]), dtw=sb("s_dtw%d" % i, [128, 64]))
                 for i in range(2)]
        preU = [U("s_pre%d" % i) for i in range(24)]
        cnt = {"tile": 0, "grp": 0}

        def bc_heads(t_, d0):
            return t_.v(t_.ap[:, d0:d0 + 8].unsqueeze(2).to_broadcast([128, 8, 64]))

        h8 = lambda t_: t_.v(t_.ap.rearrange("p (h q) -> p h q", h=8))

        def tile_front(src_d, pos, G, Sm, zero_left, zero_right, chunks, save_tile=None, load_tile=None):
            par = cnt["tile"] % 2
            cnt["tile"] += 1
            xw_ = xw[par]
            P.dma(xw_, src_d[:, :, pos * 128:pos * 128 + 132])
            emit_normmod(cx, ns, lambda c: xw_[:, c, :], 132, G, Sm, lambda c: hTw[:, c, :])
            if zero_left:
                P.memset(hTw[:, :, 0:2], 0.0)
            if zero_right:
                P.memset(hTw[:, :, 130:132], 0.0)
            sm = small[par]
            ps = cx.bank()
            for k in range(8):
                P.mm(ps[:, 0:64], hTw[:, k, 2:130], win[:, k, W1_DT:W1_DT + 64], start=(k == 0), stop=(k == 7))
            P.tt(sm["dtv"], ps[:, 0:64], dtb, ALU.add)
            P.act(sm["dtv"], sm["dtv"], AF.Exp)
            P.act(sm["dtv"], sm["dtv"], AF.Ln, bias=1.0)
            P.tt(sm["dtA"], sm["dtv"], Aneg, ALU.mult)
            ps2 = cx.bank()
            dtA = sm["dtA"]
            P.mm(ps2[:, 0:32], cx.c("LE"), dtA[:, 0:32])
            P.mm(ps2[:, 32:64], cx.c("GE"), dtA[:, 32:64])
            P.mm(ps2[:, 64:96], cx.c("GT"), dtA[:, 0:32])
            P.mm(ps2[:, 96:128], cx.c("LT"), dtA[:, 32:64])
            P.mm(ps2[:, 128:192], cx.c("ones"), dtA)
            P.act(sm["e"], ps2[:, 0:64], AF.Exp)
            P.act(sm["w"], ps2[:, 64:128], AF.Exp)
            P.act(sm["dec"], ps2[:, 128:192], AF.Exp)
            P.tt(sm["dtw"], sm["dtv"], sm["w"], ALU.mult)
            for cc in chunks:
                psp = cx.bank()
                for k in range(8):
                    P.mm(psp[:, 0:132], win[:, k, cc * 128:(cc + 1) * 128], hTw[:, k, :], start=(k == 0), stop=(k == 7))
                P.copy(T(pre.ap[:, cc, :], [preU[cc]]), psp[:, 0:132], eng="act")
            cT_ = cT[par]
            ct = lambda cc: T(cT_.ap[:, cc, :], [cTU[par][cc]])
            for cc in chunks:
                psc = cx.bank()
                for j in range(5):
                    P.mm(psc[:, 0:128], diagW[:, cc * 5 + j, :], T(pre.ap[:, cc, j:j + 128], [preU[cc]]),
                         start=(j == 0), stop=(j == 4))
                P.act(ct(cc), psc[:, 0:128], AF.Silu, bias=convb[:, cc:cc + 1], scale=1.0)
            xt_ = x_tok[par]
            if load_tile is not None:
                P.dma(T(xt_.ap, xtU[par]), xcache[load_tile])
                P.dma(B_tok[par], bcache[load_tile, :, 0:512])
                P.dma(T(cT_.ap[:, 16:20, :], cTU[par][16:20]), bcache[load_tile, :, 512:1024].rearrange("p (g t) -> p g t", g=4))
            else:
                for g in range(4):
                    pst = cx.bank()
                    for i in range(4):
                        P.mm(pst[:, i * 128:(i + 1) * 128], ct(g * 4 + i), identb)
                    P.copy(T(xt_.ap[:, g * 512:(g + 1) * 512], [xtU[par][g]]), pst, eng=("dve" if g % 2 else "act"))
                pst = cx.bank()
                for g in range(4):
                    P.mm(pst[:, g * 128:(g + 1) * 128], ct(16 + g), identb)
                P.copy(B_tok[par], pst, eng="act")
                if save_tile is not None:
                    P.dma(xcache[save_tile], T(xt_.ap, xtU[par]), q="pool")
                    P.dma(bcache[save_tile, :, 0:512], B_tok[par], q="pool")
                    P.dma(bcache[save_tile, :, 512:1024].rearrange("p (g t) -> p g t", g=4), T(cT_.ap[:, 16:20, :], cTU[par][16:20]), q="pool")
            xt = lambda g: T(xt_.ap[:, g * 512:(g + 1) * 512], [xtU[par][g]])
            bt = lambda g: B_tok[par][:, g * 128:(g + 1) * 128]
            return sm, ct, xt, bt

        def state_step(g, dr, sm, xt, bt, store_tile, xdtw_ring=None):
            xdtw_ring = xdtw_ring or xdtw
            h0 = dr * 32 + g * 8
            xw_ = xdtw_ring[cnt["grp"] % 2]
            cnt["grp"] += 1
            P.tt(h8(xw_), h8(xt(g)), bc_heads(sm["dtw"], h0), ALU.mult, eng="pool")
            ps = cx.bank()
            P.mm(ps, bt(g), xw_)
            if store_tile is not None:
                P.copy(Sbfg(g), Sg_(g), eng="act")
                P.dma(sscr[store_tile, :, g, :], Sbfg(g), q="pool")
            P.tt(h8(Sg_(g)), h8(Sg_(g)), bc_heads(sm["dec"], h0), ALU.mult)
            P.tt(Sg_(g), Sg_(g), ps, ALU.add)

        def state_tile(src_d, pos, G, Sm, dr, zl, zr, store_tile):
            sm, ct, xt, bt = tile_front(src_d, pos, G, Sm, zl, zr, range(20), save_tile=store_tile)
            for g in range(4):
                state_step(g, dr, sm, xt, bt, store_tile)

        def full_tile(t):
            sm, ct, xt, bt = tile_front(d["seq"], t, mp["G1x"], mp["S1x"], t == 0, False, range(20, 24), load_tile=t)
            P.dma(Sbin, sscr[t])
            dtA, dtv, e_ = sm["dtA"], sm["dtv"], sm["e"]
            for g in range(4):
                gp = g % 2
                Gm_, MT_, xdt_, t1_, yg_ = Gm[gp], MT[gp], xdt[gp], t1[gp], yg[gp]
                psg = cx.bank()
                P.mm(psg[:, 0:128], ct(16 + g), ct(20 + g))
                P.tt(Gm_[:, 0, :], psg[:, 0:128], cx.c("LE"), ALU.mult)
                P.tt(Gm_[:, 1, :], psg[:, 0:128], cx.c("GE"), ALU.mult)
                for drx in range(2):
                    combos = [(drx, hh) for hh in range(2)]
                    for i0, (dr, hh) in enumerate(combos):
                        i = drx * 2 + i0
                        h0 = dr * 32 + g * 8
                        tri = cx.c("LE" if dr == 0 else "GE")
                        P.tt(segrhs[i], tri.v(tri.ap.unsqueeze(1).to_broadcast([128, 4, 128])),
                             dtA.v(dtA.ap[:, h0 + hh * 4:h0 + hh * 4 + 4].unsqueeze(2).to_broadcast([128, 4, 128])), ALU.mult,
                             eng="pool")
                    pss = {}
                    for i0, (dr, hh) in enumerate(combos):
                        i = drx * 2 + i0
                        um = cx.c("GT" if dr == 0 else "LT")
                        p_ = cx.bank()
                        P.mm(p_, um, segrhs[i].v(segrhs[i].ap.rearrange("p h t -> p (h t)")))
                        pss[i] = p_
                    for i0, (dr, hh) in enumerate(combos):
                        i = drx * 2 + i0
                        P.act(Dexp[i], pss[i].v(pss[i].ap.rearrange("p (h t) -> p h t", h=4)), AF.Exp)
                    for i0, (dr, hh) in enumerate(combos):
                        i = drx * 2 + i0
                        P.tt(MT_[:, dr * 8 + hh * 4:dr * 8 + hh * 4 + 4, :], Dexp[i],
                             Gm_.v(Gm_.ap[:, dr, :].unsqueeze(1).to_broadcast([128, 4, 128])), ALU.mult)
                for dr in range(2):
                    P.tt(h8(xdt_[:, dr, :]), h8(xt(g)), bc_heads(dtv, dr * 32 + g * 8), ALU.mult, eng="pool")
                psy = cx.bank()
                for hh in range(8):
                    P.mm(psy[:, hh * 64:(hh + 1) * 64], MT_[:, hh, :], xdt_[:, 0, hh * 64:(hh + 1) * 64], start=True, stop=False)
                    P.mm(psy[:, hh * 64:(hh + 1) * 64], MT_[:, 8 + hh, :], xdt_[:, 1, hh * 64:(hh + 1) * 64], start=False, stop=True)
                P.copy(Sbfg(g), Sg_(g), eng="act")
                pso = cx.bank()
                P.mm(pso, ct(20 + g), Sbfg(g))
                pso2 = cx.bank()
                P.mm(pso2, ct(20 + g), Sbin[:, g, :])
                P.tt(h8(t1_), h8(pso), bc_heads(e_, 32 + g * 8), ALU.mult)
                P.tt(yg_, psy, t1_, ALU.add)
                P.tt(h8(t1_), h8(pso2), bc_heads(e_, g * 8), ALU.mult)
                P.tt(yg_, yg_, t1_, ALU.add)
                P.tt(h8(t1_), h8(xt(g)), dsk.v(dsk.ap[:, g * 8:g * 8 + 8].unsqueeze(2).to_broadcast([128, 8, 64])), ALU.mult)
                P.tt(yg_, yg_, t1_, ALU.add)
                P.dma(yscr[t, :, g * 512:(g + 1) * 512], yg_, q="pool")
                state_step(g, 1, sm, xt, bt, None)

        P.memset(T(S.ap, SU), 0.0)
        with contextlib.ExitStack() as sA:
            sba = lambda name, shape, dt=F32: P.sb(sA, name, shape, dt)
            nsA = NormScratch(P, sA, 260)
            xwA = [sba("A_xw%d" % i, [128, 8, 260]) for i in range(2)]
            hTwA = sba("A_hTw", [128, 8, 260], BF16)
            preA = sba("A_pre", [128, 20, 260], BF16)
            cTA = [sba("A_cT%d" % i, [128, 20, 256], BF16) for i in range(2)]
            cTAU = [[U("A_cT%d_%d" % (i, c)) for c in range(20)] for i in range(2)]
            xtA = [sba("A_xt%d" % i, [128, 2, 2048], BF16) for i in range(2)]
            xtAU = [[[U("A_xt%d_%d_%d" % (i, ti, g)) for g in range(4)] for ti in range(2)] for i in range(2)]
            btA = [sba("A_bt%d" % i, [128, 2, 512], BF16) for i in range(2)]
            btAU = [[U("A_bt%d_%d" % (i, ti)) for ti in range(2)] for i in range(2)]
            xdtwA = [sba("A_xdtw%d" % i, [128, 512], BF16) for i in range(2)]
            npair = [0]

            def pair_front(src_d, pos0, G, Sm, zero_left, zero_right, save_tiles):
                par = npair[0] % 2
                npair[0] += 1
                xw_ = xwA[par]
                P.dma(xw_, src_d[:, :, pos0 * 128:pos0 * 128 + 260])
                emit_normmod(cx, nsA, lambda c: xw_[:, c, :], 260, G, Sm, lambda c: hTwA[:, c, :])
                if zero_left:
                    P.memset(hTwA[:, :, 0:2], 0.0)
                if zero_right:
                    P.memset(hTwA[:, :, 258:260], 0.0)
                for ti in range(2):
                    sm = small[ti]
                    ps = cx.bank()
                    for k in range(8):
                        P.mm(ps[:, 0:64], hTwA[:, k, 2 + ti * 128:2 + (ti + 1) * 128], win[:, k, W1_DT:W1_DT + 64],
                             start=(k == 0), stop=(k == 7))
                    P.tt(sm["dtv"], ps[:, 0:64], dtb, ALU.add)
                    P.act(sm["dtv"], sm["dtv"], AF.Exp)
                    P.act(sm["dtv"], sm["dtv"], AF.Ln, bias=1.0)
                    P.tt(sm["dtA"], sm["dtv"], Aneg, ALU.mult)
                    ps2 = cx.bank()
                    dtA = sm["dtA"]
                    P.mm(ps2[:, 64:96], cx.c("GT"), dtA[:, 0:32])
                    P.mm(ps2[:, 128:192], cx.c("ones"), dtA)
                    P.act(sm["w"][:, 0:32], ps2[:, 64:96], AF.Exp)
                    P.act(sm["dec"], ps2[:, 128:192], AF.Exp)
                    P.tt(sm["dtw"][:, 0:32], sm["dtv"][:, 0:32], sm["w"][:, 0:32], ALU.mult)
                for cc in range(20):
                    psp = cx.bank()
                    for k in range(8):
                        P.mm(psp[:, 0:260], win[:, k, cc * 128:(cc + 1) * 128], hTwA[:, k, :], start=(k == 0), stop=(k == 7))
                    P.copy(T(preA.ap[:, cc, :], [preU[cc]]), psp[:, 0:260], eng="act")
                cT_ = cTA[par]
                ctf = lambda cc: T(cT_.ap[:, cc, :], [cTAU[par][cc]])
                for cc in range(20):
                    psc = cx.bank()
                    for j in range(5):
                        P.mm(psc[:, 0:256], diagW[:, cc * 5 + j, :], T(preA.ap[:, cc, j:j + 256], [preU[cc]]),
                             start=(j == 0), stop=(j == 4))
                    P.act(ctf(cc), psc[:, 0:256], AF.Silu, bias=convb[:, cc:cc + 1], scale=1.0)
                out = []
                for ti in range(2):
                    cts = lambda cc, ti=ti: T(cT_.ap[:, cc, ti * 128:(ti + 1) * 128], [cTAU[par][cc]])
                    xtt = lambda g, ti=ti: T(xtA[par].ap[:, ti, g * 512:(g + 1) * 512], [xtAU[par][ti][g]])
                    btt_all = T(btA[par].ap[:, ti, :], [btAU[par][ti]])
                    for g in range(4):
                        pst = cx.bank()
                        for i in range(4):
                            P.mm(pst[:, i * 128:(i + 1) * 128], cts(g * 4 + i), identb)
                        P.copy(xtt(g), pst, eng=("dve" if g % 2 else "act"))
                    pst = cx.bank()
                    for g in range(4):
                        P.mm(pst[:, g * 128:(g + 1) * 128], cts(16 + g), identb)
                    P.copy(btt_all, pst, eng="act")
                    if save_tiles is not None:
                        tt_ = save_tiles[ti]
                        P.dma(xcache[tt_], T(xtA[par].ap[:, ti, :], xtAU[par][ti]), q="pool")
                        P.dma(bcache[tt_, :, 0:512], btt_all, q="pool")
                        P.dma(bcache[tt_, :, 512:1024].rearrange("p (g t) -> p g t", g=4),
                              T(cT_.ap[:, 16:20, ti * 128:(ti + 1) * 128], cTAU[par][16:20]), q="pool")
                    btt = lambda g, ti=ti: T(btA[par].ap[:, ti, g * 128:(g + 1) * 128], [btAU[par][ti]])
                    out.append((small[ti], xtt, btt))
                return out

            def state_pair(src_d, pos0, G, Sm, zl, zr, store_tiles):
                tiles = pair_front(src_d, pos0, G, Sm, zl, zr, store_tiles)
                for ti, (sm, xtt, btt) in enumerate(tiles):
                    for g in range(4):
                        state_step(g, 0, sm, xtt, btt, None if store_tiles is None else store_tiles[ti], xdtw_ring=xdtwA)

            state_pair(d["xc1"], 0, mp["G1c"], mp["S1c"], True, True, None)
            for pos0 in range(0, 16, 2):
                state_pair(d["seq"], pos0, mp["G1x"], mp["S1x"], pos0 == 0, False, (pos0, pos0 + 1))
            P.flush()
        Sbin = sb("s_Sbin", [128, 4, 512], BF16)
        ns = NormScratch(P, s2, 132)
        xw = [sb("s_xw%d" % i, [128, 8, 132]) for i in range(2)]
        hTw = sb("s_hTw", [128, 8, 132], BF16)
        pre = sb("s_pre", [128, 24, 132], BF16)
        cT = [sb("s_cT%d" % i, [128, 24, 128], BF16) for i in range(2)]
        cTU = [[U("s_cT%d_%d" % (i, c)) for c in range(24)] for i in range(2)]
        x_tok = [sb("s_xtok%d" % i, [128, 2048], BF16) for i in range(2)]
        xtU = [[U("s_xt%d_%d" % (i, g)) for g in range(4)] for i in range(2)]
        B_tok = [sb("s_Btok%d" % i, [128, 512], BF16) for i in range(2)]
        segrhs = [sb("s_segrhs%d" % i, [128, 4, 128]) for i in range(4)]
        Dexp = [sb("s_Dexp%d" % i, [128, 4, 128]) for i in range(4)]
        MT = [sb("s_MT%d" % i, [128, 16, 128], BF16) for i in range(2)]
        Gm = [sb("s_Gm%d" % i, [128, 2, 128]) for i in range(2)]
        xdt = [sb("s_xdt%d" % i, [128, 2, 512], BF16) for i in range(2)]
        xdtw = [sb("s_xdtw%d" % i, [128, 512], BF16) for i in range(2)]
        t1 = [sb("s_t1%d" % i, [128, 512]) for i in range(2)]
        yg = [sb("s_yg%d" % i, [128, 512]) for i in range(2)]
        sxm = T(sx_mine, U("d_sxm"))
        sxg = T(sx_gath, U("d_sxg"))
        P.dma(sxm, T(S.ap.rearrange("p g q -> p (g q)"), SU), q="pool")
        P.collective("AllGather", [[0, 1], [2, 3], [4, 5], [6, 7]], sxg, sxm)
        P.flush()
        r0 = sb("s_r0", [128, 2048])
        r1 = sb("s_r1", [128, 2048])
        pmk = sb("s_pmk", [128, 2])
        P.dma(pmk, pmask_d)
        P.dma(r0, sx_gath[0:128, :])
        P.dma(r1, sx_gath[128:256, :])
        Sall = T(S.ap.rearrange("p g q -> p (g q)"), SU)
        P.ts(r0, r0, pmk[:, 0:1], ALU.mult)
        P.stt(Sall, r1, pmk[:, 1:2], r0, ALU.mult, ALU.add)
        for t in range(15, -1, -1):
            full_tile(t)
        P.flush()


def emit_l1_gate_out(cx, st, d, mp, yscr, xmid):
    P = cx.P
    with contextlib.ExitStack() as s2:
        sb = lambda name, shape, dt=F32: P.sb(s2, name, shape, dt)
        wz = sb("g_wz", [128, 8, 2048], BF16)
        wout = sb("g_wout", [128, 16, 1024], BF16)
        ng1 = sb("g_ng1", [128, 16])
        for k in range(8):
            P.dma(wz[:, k, :], d["w_in"][:, k, W1_Z:W1_END], q="pool")
        for k in range(0, 16, 2):
            P.dma(wout[:, k:k + 2, :], d["w_out"][:, k:k + 2, :], q="pool")
        P.dma(ng1, d["ng1"])
        for k in range(16):
            P.ts(wout[:, k, :], wout[:, k, :], ng1[:, k:k + 1], ALU.mult)
        nss = [NormScratch(P, s2, 128) for _ in range(2)]
        bufs = [dict(xw=sb("g_xw%d" % i, [128, 8, 128]), hT=sb("g_hT%d" % i, [128, 8, 128], BF16), yt=sb("g_y%d" % i, [128, 2048]),
                     zs=[sb("g_zs%d_%d" % (i, g), [128, 512]) for g in range(4)],
                     sq=[sb("g_sq%d_%d" % (i, g), [128, 512], BF16) for g in range(4)], yn=sb("g_yn%d" % i, [128, 2048], BF16),
                     ynT=sb("g_ynT%d" % i, [128, 16, 128], BF16), st4=sb("g_st%d" % i, [128, 8]), xo=sb("g_xo%d" % i, [128, 8, 128]))
                for i in range(2)]
        identb = cx.c("ident", bf=True)
        for t in range(DBG.get("l1_tiles", 16)):
            bb = bufs[t % 2]
            ns = nss[t % 2]
            xw, hT, yt, zs, sq, yn, ynT, st4, xo = (bb[k] for k in ("xw", "hT", "yt", "zs", "sq", "yn", "ynT", "st4", "xo"))
            P.dma(xw, d["seq"][:, :, 2 + t * 128:2 + (t + 1) * 128])
            P.dma(yt, yscr[t])
            emit_normmod(cx, ns, lambda c: xw[:, c, :], 128, mp["G1x"], mp["S1x"], lambda c: hT[:, c, :])
            P.memset(st4, 0.0)
            for g in range(4):
                ps = cx.bank()
                for k in range(8):
                    P.mm(ps, hT[:, k, :], wz[:, k, g * 512:(g + 1) * 512], start=(k == 0), stop=(k == 7))
                P.act(zs[g], ps, AF.Silu)
                P.tt(yt[:, g * 512:(g + 1) * 512], yt[:, g * 512:(g + 1) * 512], zs[g], ALU.mult)
                P.act(sq[g], yt[:, g * 512:(g + 1) * 512], AF.Square, accum=st4[:, g:g + 1])
            P.act(st4[:, 4:8], st4[:, 0:4], AF.Sqrt, bias=EPS, scale=1.0 / 512)
            P.recip(st4[:, 4:8], st4[:, 4:8])
            P.tt(yn.v(yn.ap.rearrange("p (g q) -> p g q", g=4)), yt.v(yt.ap.rearrange("p (g q) -> p g q", g=4)),
                 st4.v(st4.ap[:, 4:8].unsqueeze(2).to_broadcast([128, 4, 512])), ALU.mult)
            for q in range(4):
                pst = cx.bank()
                for i in range(4):
                    k = q * 4 + i
                    P.mm(pst[:, i * 128:(i + 1) * 128], yn[:, k * 128:(k + 1) * 128], identb)
                P.copy(ynT.v(ynT.ap[:, q * 4:q * 4 + 4, :]), pst.v(pst.ap.rearrange("p (i t) -> p i t", i=4)), eng="act")
            for half in range(2):
                psy = cx.bank()
                for cc in range(4):
                    c = half * 4 + cc
                    for k in range(16):
                        P.mm(psy[:, cc * 128:(cc + 1) * 128], wout[:, k, c * 128:(c + 1) * 128], ynT[:, k, :],
                             start=(k == 0), stop=(k == 15))
                for cc in range(4):
                    c = half * 4 + cc
                    P.stt(xo[:, c, :], psy[:, cc * 128:(cc + 1) * 128], mp["g1x"][:, c:c + 1], xw[:, c, :], ALU.mult, ALU.add)
            P.dma(xmid[:, :, t * 128:(t + 1) * 128], xo, q="pool")
        P.flush()


def emit_l1_ffn_final(cx, st, d, mp, xmid, y_out):
    P = cx.P
    fng = P.sb(st, "fng", [128, 8], F32)
    zer = P.sb(st, "fzero", [128, 8], F32)
    P.dma(fng, d["fng"])
    P.memset(zer, 0.0)
    for sgi in range(2):
        with contextlib.ExitStack() as s2:
            xr = P.sb(s2, "c_x", [128, 8, 1024], F32)
            xb = Buf(xr.ap, "c_x%d" % sgi, 1024, 512)
            for g in range(2):
                P.dma(xb.tok(g * 512, (g + 1) * 512), xmid[:, :, sgi * 1024 + g * 512:sgi * 1024 + (g + 1) * 512])
            segs = [(lambda c, a=g * 512: xb.tok(a, a + 512, sel=(slice(None), c)), 512, mp["G2x"], mp["S2x"], mp["g2x"])
                    for g in range(2)]
            emit_ffn(cx, s2, d["f_in"], d["f_out"], segs, None)
            ns = NormScratch(P, s2, 512)
            o = P.sb(s2, "c_o", [128, 8, 512], F32)
            for g in range(2):
                emit_normmod(cx, ns, lambda c, a=g * 512: xb.tok(a, a + 512, sel=(slice(None), c)), 512, fng, zer,
                             lambda c: o[:, c, :])
                P.dma(y_out[:, :, sgi * 1024 + g * 512:sgi * 1024 + (g + 1) * 512], o)
            P.flush()


def l1_body(cx, st, d, sscr, yscr, xmid, y_out, stop_after=None):
    P = cx.P
    modv = P.sb(st, "modv1", [128, 48, 2], F32)
    ngs = P.sb(st, "ngs1", [128, 2, 8], F32)
    P.dma(ngs, d["ng"])
    emit_mod(cx, st, d["cvec"], d["modw"], d["modb"], modv)
    mp = emit_modparams(cx, st, modv, ngs, "l1")
    emit_l1_ssd(cx, st, d, mp, sscr, yscr)
    if stop_after == "ssd":
        return
    emit_l1_gate_out(cx, st, d, mp, yscr, xmid)
    if stop_after == "gate":
        return
    emit_l1_ffn_final(cx, st, d, mp, xmid, y_out)


def l1_scratch(nc):
    sscr = nc.dram_tensor("sscr", [16, 128, 4, 512], BF16, kind="Internal").ap()
    dk = "ExternalOutput" if DBG.get("l1dump") else "Internal"
    yscr = nc.dram_tensor("yscr", [16, 128, 2048], F32, kind=dk).ap()
    xmid = nc.dram_tensor("xmid", [128, 8, NTOK], F32, kind=dk).ap()
    return sscr, yscr, xmid


def build_l1(stop_after=None):
    nc = bass.Bass("TRN2", target_bir_lowering=False)
    d = {k: nc.dram_tensor(k, list(s), F32, kind="ExternalInput").ap() for k, s in L1_SHAPES.items()}
    y_out = nc.dram_tensor("y_out", [128, 8, NTOK], F32, kind="ExternalOutput").ap()
    sscr, yscr, xmid = l1_scratch(nc)
    with contextlib.ExitStack() as st:
        P = Prog(nc, st)
        cx = Ctx(nc, P, st)
        cx.cst_d = d["cst"]
        load_consts(cx)
        l1_body(cx, st, d, sscr, yscr, xmid, y_out, stop_after)
    return nc


def make_pm(half):
    pm = np.zeros((128, 64), np.float32)
    r = np.arange(32)
    own_base = 32 * half
    oth_base = 64 + 32 * (1 - half)
    pm[own_base + r, r] = 1.0
    pm[oth_base + r, 63 - r] = 1.0
    return pm


def emit_exchange(cx, st, xres, cres, mine, gaths, seq, xc1, pm_d):
    P = cx.P
    mineU = [U("d_mine%d" % i) for i in range(16)]
    gathT = [T(g, U("d_gath%d" % i)) for i, g in enumerate(gaths)]
    ident = cx.c("ident")
    mine_v = mine.rearrange("(c q) n -> q c n", q=32)
    with contextlib.ExitStack() as s2:
        tok = [P.sb(s2, "x_tok%d" % i, [128, 1024], F32) for i in range(2)]
        zt = P.sb(s2, "x_zero", [128, 8, 2], F32)
        P.memset(zt, 0.0)
        P.dma(seq[:, :, 0:2], zt)
        P.dma(seq[:, :, NSEQ - 2:NSEQ], zt)
        P.dma(xc1[:, :, 0:2], zt)
        P.dma(xc1[:, :, NCTXP - 2:NCTXP], zt)
        P.dma(xc1[:, :, 2:2 + NCTX], cres.all())
        for t in range(16):
            tk = tok[t % 2]
            for hb in range(2):
                ps = cx.bank()
                for cc in range(4):
                    kc = hb * 4 + cc
                    P.mm(ps[:, cc * 128:(cc + 1) * 128], xres.tok(t * 128, (t + 1) * 128, sel=(slice(None), kc)), ident)
                P.copy(tk[:, hb * 512:(hb + 1) * 512], ps, eng=("act" if hb else "dve"))
            for rr in range(2):
                P.dma(T(mine_v[2 * t + rr], [mineU[t]]), tk[rr * 64:(rr + 1) * 64, :], q=("pool" if rr else "sp"))
        for q in range(4):
            P.collective("AllGather", [[0, 1], [2, 3], [4, 5], [6, 7]], gathT[q],
                         T(mine[q * 512:(q + 1) * 512, :], mineU))
        P.flush()
    with contextlib.ExitStack() as s2:
        pm = P.sb(s2, "x_pm", [128, 64], F32)
        P.dma(pm, pm_d)
        NXS = 12
        xs = [P.sb(s2, "x_st%d" % i, [128, 1024], F32) for i in range(NXS)]
        xsU = [[U("x_st%d_%d" % (i, q)) for q in range(4)] for i in range(NXS)]
        stg = [P.sb(s2, "x_stg%d" % i, [128, 8, 512], F32) for i in range(2)]

        def col_rows(r, c):
            q, w = c // 16, c % 16
            return gathT[q].v(gaths[q][r * 512 + w * 32:r * 512 + (w + 1) * 32, :])

        for cj in range(64):
            xa = xs[cj % NXS].ap
            xu = xsU[cj % NXS]
            for r in range(2):
                P.dma(T(xa[r * 32:(r + 1) * 32, :], [xu[r]]), col_rows(r, cj))
                P.dma(T(xa[64 + r * 32:64 + (r + 1) * 32, :], [xu[2 + r]]), col_rows(r, 63 - cj))
            ps = cx.bank()
            for kc in range(8):
                P.mm(ps[:, kc * 64:(kc + 1) * 64], T(xa[:, kc * 128:(kc + 1) * 128], xu), pm)
            sg = stg[(cj // 8) % 2]
            P.copy(sg[:, :, (cj % 8) * 64:(cj % 8 + 1) * 64], ps.v(ps.ap.rearrange("p (k t) -> p k t", k=8)),
                   eng=("act" if cj % 2 else "dve"))
            if cj % 8 == 7:
                P.dma(seq[:, :, 2 + (cj - 7) * 64:2 + (cj + 1) * 64], sg, q="pool")
        P.flush()


FUSED_L1_KEYS = [k for k in L1_SHAPES if k not in ("seq", "xc1", "cst")]


def build_fused():
    nc = bass.Bass("TRN2", target_bir_lowering=False)
    d0 = {k: nc.dram_tensor(k, list(s), F32, kind="ExternalInput").ap() for k, s in L0_SHAPES.items()}
    d1 = {k: nc.dram_tensor("l1_" + k, list(L1_SHAPES[k]), F32, kind="ExternalInput").ap() for k in FUSED_L1_KEYS}
    pm_d = nc.dram_tensor("pm", [128, 64], F32, kind="ExternalInput").ap()
    d1["pmask"] = nc.dram_tensor("pmask", [128, 2], F32, kind="ExternalInput").ap()
    d0["pmask"] = d1["pmask"]
    d1["cst"] = d0["cst"]
    d1["seq"] = nc.dram_tensor("seq", [128, 8, NSEQ], F32, kind="Internal").ap()
    d1["xc1"] = nc.dram_tensor("xc1", [128, 8, NCTXP], F32, kind="Internal").ap()
    mine = nc.dram_tensor("ex_mine", [NTOK, 1024], F32, kind="Internal").ap()
    gath = [nc.dram_tensor("ex_gath%d" % q, [1024, 1024], F32, kind="Internal").ap() for q in range(4)]
    y_out = nc.dram_tensor("y_out", [128, 8, NTOK], F32, kind="ExternalOutput").ap()
    sscr, yscr, xmid = l1_scratch(nc)
    with contextlib.ExitStack() as st:
        P = Prog(nc, st)
        cx = Ctx(nc, P, st)
        cx.cst_d = d0["cst"]
        load_consts(cx)
        with contextlib.ExitStack() as s0:
            xres, cres = l0_body(cx, s0, d0)
            emit_exchange(cx, s0, xres, cres, mine, gath, d1["seq"], d1["xc1"], pm_d)
        l1_body(cx, st, d1, sscr, yscr, xmid, y_out)
    return nc


def prep_fused(inp, b, half):
    m = prep_l0(inp, b, half)
    d1 = prep_l1(inp, None, None, b, half)
    for k in FUSED_L1_KEYS:
        m["l1_" + k] = d1[k]
    m["pm"] = make_pm(half)
    pmask = np.zeros((128, 2), np.float32)
    pmask[:, 1 - half] = 1.0
    m["pmask"] = pmask
    return m


def _run(nc, maps):
    return run_bass_kernel_spmd(nc, maps, core_ids=list(range(8))).results


def kernel_unfused(**inputs):
    inp = {k: np.asarray(v) for k, v in inputs.items()}
    res0 = _run(build_l0(), [prep_l0(inp, c // 2, c % 2) for c in range(8)])
    x1 = np.zeros((4, 4096, 1024), np.float32)
    c1 = np.zeros((4, NCTX, 1024), np.float32)
    for c in range(8):
        b, half = c // 2, c % 2
        xo = unfm(res0[c]["x1"])
        if half == 1:
            xo = xo[::-1]
        x1[b, half * NTOK:(half + 1) * NTOK] = xo
        if half == 0:
            c1[b] = unfm(res0[c]["c1"])
    res1 = _run(build_l1(), [prep_l1(inp, x1[c // 2], c1[c // 2], c // 2, c % 2) for c in range(8)])
    out = np.zeros((4, 4096, 1024), np.float32)
    for b in range(4):
        xcm = np.zeros((4096, 1024), np.float32)
        for half in range(2):
            yo = unfm(res1[b * 2 + half]["y_out"])
            if half == 1:
                yo = yo[::-1]
            xcm[half * NTOK:(half + 1) * NTOK] = yo
        out[b] = xcm.reshape(64, 64, 1024).transpose(1, 0, 2).reshape(4096, 1024)
    return out


def assemble(res1):
    out = np.zeros((4, 4096, 1024), np.float32)
    for b in range(4):
        xcm = np.zeros((4096, 1024), np.float32)
        for half in range(2):
            yo = unfm(res1[b * 2 + half]["y_out"])
            if half == 1:
                yo = yo[::-1]
            xcm[half * NTOK:(half + 1) * NTOK] = yo
        out[b] = xcm.reshape(64, 64, 1024).transpose(1, 0, 2).reshape(4096, 1024)
    return out


def kernel(**inputs):
    inp = {k: np.asarray(v) for k, v in inputs.items()}
    res = _run(build_fused(), [prep_fused(inp, c // 2, c % 2) for c in range(8)])
    return assemble(res)
```
